# Optimizing a Trainium2 kernel written in Bass

```python
import jax, jax.numpy as jnp
from jax import lax
import numpy as np

D_MODEL = 2048
BATCH = 4
SEQ = 2048
DEPTH = 4

FOX_W = D_MODEL // 4
FOX_HD = 64
FOX_HEADS = FOX_W // FOX_HD
FOX_BLOCK = 128
GLA_HEADS = 4
GLA_W = D_MODEL // 4
GLA_DV = GLA_W // GLA_HEADS
GLA_DK = GLA_DV // 2
GLA_KW = GLA_HEADS * GLA_DK
GLA_RANK = 16
GLA_TAU = 16.0
GLA_CHUNK = 32
LRU_W = D_MODEL // 2
LRU_BLOCKS = 16
LRU_BW = LRU_W // LRU_BLOCKS
LRU_C = 8.0
CONV_WIDTH = 4
D_MIX = FOX_W + GLA_W + LRU_W
IN_W = 3 * FOX_W + FOX_HEADS + 2 * GLA_KW + 2 * GLA_W + GLA_RANK + 2 * LRU_W
FFN_HIDDEN = -(-8 * D_MODEL // 768) * 256
RMS_EPS = 1e-6

kernel_name = "hymba_fox_gla_rglru_hybrid"


def in_proj_sizes():
    return [FOX_W, FOX_W, FOX_W, FOX_HEADS,
            GLA_KW, GLA_KW, GLA_W, GLA_W, GLA_RANK,
            LRU_W, LRU_W]


def rmsnorm(x, g):
    xf = x.astype(jnp.float32)
    y = xf * lax.rsqrt(jnp.mean(xf * xf, axis=-1, keepdims=True) + RMS_EPS)
    return (y * g.astype(jnp.float32)).astype(x.dtype)


def fox_attention(q, k, v, f_logit, f_bias):
    B, S, H, Dh = q.shape
    c = jnp.cumsum(jax.nn.log_sigmoid((f_logit + f_bias).astype(jnp.float32)), axis=1)
    c = c.transpose(0, 2, 1)
    q, k, v = (t.transpose(0, 2, 1, 3) for t in (q, k, v))
    scale = Dh ** -0.5
    outs = []
    for i in range(S // FOX_BLOCK):
        q0 = i * FOX_BLOCK
        end = q0 + FOX_BLOCK
        s = jnp.einsum('bhqd,bhkd->bhqk', q[:, :, q0:end], k[:, :, :end]).astype(jnp.float32) * scale
        s = s + c[:, :, q0:end, None] - c[:, :, None, :end]
        mask = jnp.arange(end)[None, :] <= (q0 + jnp.arange(FOX_BLOCK))[:, None]
        p = jax.nn.softmax(jnp.where(mask, s, -jnp.inf), axis=-1)
        outs.append(jnp.einsum('bhqk,bhkd->bhqd', p.astype(v.dtype), v[:, :, :end]))
    o = jnp.concatenate(outs, axis=2)
    return o.transpose(0, 2, 1, 3).reshape(B, S, H * Dh)


def gla_chunked(q, k, v, log_alpha):
    B, S, H, Dk = q.shape
    Dv = v.shape[-1]
    C = GLA_CHUNK
    N = S // C

    def chunk(t):
        return t.astype(jnp.float32).reshape(B, N, C, H, t.shape[-1]).transpose(0, 3, 1, 2, 4)

    qc = chunk(q) * (Dk ** -0.5)
    kc = chunk(k)
    vc = chunk(v)
    b = jnp.cumsum(chunk(log_alpha), axis=3)
    causal = jnp.tril(jnp.ones((C, C), dtype=bool))
    diff = b[:, :, :, :, None, :] - b[:, :, :, None, :, :]
    decay = jnp.exp(jnp.where(causal[:, :, None], diff, -jnp.inf))
    A = jnp.einsum('bhntd,bhnsd,bhntsd->bhnts', qc, kc, decay)
    o_intra = jnp.einsum('bhnts,bhnsv->bhntv', A, vc)
    b_last = b[:, :, :, -1, :]
    q_in = qc * jnp.exp(b)
    k_st = kc * jnp.exp(b_last[:, :, :, None, :] - b)
    U = jnp.einsum('bhncd,bhncv->bhndv', k_st, vc)

    def step(state, inp):
        dec, u = inp
        return dec[..., None] * state + u, state

    s0 = jnp.zeros((B, H, Dk, Dv), jnp.float32)
    _, s_prev = lax.scan(step, s0, (jnp.moveaxis(jnp.exp(b_last), 2, 0), jnp.moveaxis(U, 2, 0)))
    s_prev = jnp.moveaxis(s_prev, 0, 2)
    o_inter = jnp.einsum('bhncd,bhndv->bhncv', q_in, s_prev)
    o = o_intra + o_inter
    return o.transpose(0, 2, 3, 1, 4).reshape(B, S, H, Dv)


def causal_depthwise_conv(x, w, b):
    K = w.shape[0]
    S = x.shape[1]
    xp = jnp.pad(x, ((0, 0), (K - 1, 0), (0, 0)))
    out = b
    for j in range(K):
        out = out + xp[:, j:j + S] * w[j]
    return out


def rg_lru(x, w_a, b_a, w_i, b_i, lam):
    B, S, W = x.shape
    xb = x.reshape(B, S, LRU_BLOCKS, LRU_BW)
    r = jax.nn.sigmoid(jnp.einsum('bsnd,nde->bsne', xb, w_a).reshape(B, S, W) + b_a)
    i = jax.nn.sigmoid(jnp.einsum('bsnd,nde->bsne', xb, w_i).reshape(B, S, W) + b_i)
    log_a = (LRU_C * r.astype(jnp.float32)) * jax.nn.log_sigmoid(lam.astype(jnp.float32))
    a = jnp.exp(log_a)
    u = jnp.sqrt(-jnp.expm1(2.0 * log_a)) * (i * x).astype(jnp.float32)

    def step(h, inp):
        a_t, u_t = inp
        h = a_t * h + u_t
        return h, h

    _, hs = lax.scan(step, jnp.zeros((B, W), jnp.float32), (a.swapaxes(0, 1), u.swapaxes(0, 1)))
    return hs.swapaxes(0, 1)


def setup_inputs(seed: int = 0) -> dict:
    key = jax.random.key(seed)
    ks = jax.random.split(key, 24)
    f32 = jnp.float32

    def nrm(k, shape, scale):
        return jax.random.normal(k, shape, f32) * scale

    def gain(k, shape):
        return 1.0 + 0.02 * jax.random.normal(k, shape, f32)

    u = jax.random.uniform(ks[15], (DEPTH, LRU_W), f32, minval=0.9, maxval=0.999)
    a_base = u ** (1.0 / LRU_C)
    lru_lambda = jnp.log(a_base) - jnp.log1p(-a_base)
    return {
        "x": nrm(ks[0], (BATCH, SEQ, D_MODEL), 1.0),
        "norm_mix": gain(ks[1], (DEPTH, D_MODEL)),
        "w_in": nrm(ks[2], (DEPTH, D_MODEL, IN_W), D_MODEL ** -0.5),
        "fox_f_bias": 2.0 + 0.1 * jax.random.normal(ks[3], (DEPTH, FOX_HEADS), f32),
        "fox_out_norm": gain(ks[4], (DEPTH, FOX_W)),
        "gla_gate_w2": nrm(ks[5], (DEPTH, GLA_RANK, GLA_KW), GLA_RANK ** -0.5),
        "gla_gate_bias": nrm(ks[6], (DEPTH, GLA_KW), 0.1),
        "gla_head_norm": gain(ks[7], (DEPTH, GLA_DV)),
        "conv_w": nrm(ks[8], (DEPTH, CONV_WIDTH, LRU_W), CONV_WIDTH ** -0.5),
        "conv_b": nrm(ks[9], (DEPTH, LRU_W), 0.02),
        "lru_w_a": nrm(ks[10], (DEPTH, LRU_BLOCKS, LRU_BW, LRU_BW), LRU_BW ** -0.5),
        "lru_b_a": nrm(ks[11], (DEPTH, LRU_W), 0.1),
        "lru_w_i": nrm(ks[12], (DEPTH, LRU_BLOCKS, LRU_BW, LRU_BW), LRU_BW ** -0.5),
        "lru_b_i": nrm(ks[13], (DEPTH, LRU_W), 0.1),
        "lru_lambda": lru_lambda,
        "lru_out_norm": gain(ks[14], (DEPTH, LRU_W)),
        "w_out": nrm(ks[16], (DEPTH, D_MIX, D_MODEL), D_MIX ** -0.5),
        "norm_ffn": gain(ks[17], (DEPTH, D_MODEL)),
        "w_gate": nrm(ks[18], (DEPTH, D_MODEL, FFN_HIDDEN), D_MODEL ** -0.5),
        "w_up": nrm(ks[19], (DEPTH, D_MODEL, FFN_HIDDEN), D_MODEL ** -0.5),
        "w_down": nrm(ks[20], (DEPTH, FFN_HIDDEN, D_MODEL), FFN_HIDDEN ** -0.5),
        "final_norm": gain(ks[21], (D_MODEL,)),
    }


def reference(x, norm_mix, w_in, fox_f_bias, fox_out_norm, gla_gate_w2, gla_gate_bias,
              gla_head_norm, conv_w, conv_b, lru_w_a, lru_b_a, lru_w_i, lru_b_i, lru_lambda,
              lru_out_norm, w_out, norm_ffn, w_gate, w_up, w_down, final_norm):
    B, S, _ = x.shape
    offsets = [int(o) for o in np.cumsum(in_proj_sizes())[:-1]]
    for l in range(DEPTH):
        h = rmsnorm(x, norm_mix[l])
        proj = h @ w_in[l]
        fq, fk, fv, ff, gq, gk, gv, gg, gr, lg, lx = jnp.split(proj, offsets, axis=-1)
        fox = fox_attention(fq.reshape(B, S, FOX_HEADS, FOX_HD), fk.reshape(B, S, FOX_HEADS, FOX_HD),
                            fv.reshape(B, S, FOX_HEADS, FOX_HD), ff, fox_f_bias[l])
        fox = rmsnorm(fox, fox_out_norm[l])
        log_alpha = jax.nn.log_sigmoid((gr @ gla_gate_w2[l] + gla_gate_bias[l]).astype(jnp.float32)) / GLA_TAU
        gla = gla_chunked(gq.reshape(B, S, GLA_HEADS, GLA_DK), gk.reshape(B, S, GLA_HEADS, GLA_DK),
                          gv.reshape(B, S, GLA_HEADS, GLA_DV), log_alpha.reshape(B, S, GLA_HEADS, GLA_DK))
        gla = rmsnorm(gla, gla_head_norm[l]) * jax.nn.silu(gg.reshape(B, S, GLA_HEADS, GLA_DV).astype(jnp.float32))
        gla = gla.reshape(B, S, GLA_W).astype(h.dtype)
        lru_in = causal_depthwise_conv(lx, conv_w[l], conv_b[l])
        lru = rg_lru(lru_in, lru_w_a[l], lru_b_a[l], lru_w_i[l], lru_b_i[l], lru_lambda[l])
        lru = rmsnorm((lru * jax.nn.gelu(lg.astype(jnp.float32))).astype(h.dtype), lru_out_norm[l])
        mix = jnp.concatenate([fox.astype(h.dtype), gla, lru], axis=-1) @ w_out[l]
        x = x + mix.astype(x.dtype)
        h = rmsnorm(x, norm_ffn[l])
        ffn = (jax.nn.silu(h @ w_gate[l]) * (h @ w_up[l])) @ w_down[l]
        x = x + ffn.astype(x.dtype)
    return rmsnorm(x, final_norm)
```

```python
import contextlib
import numpy as np
import concourse.bass as bass
import concourse.mybir as mybir
from concourse.bass_utils import run_bass_kernel_spmd

F32 = mybir.dt.float32
BF16 = mybir.dt.bfloat16
AF = mybir.ActivationFunctionType
ALU = mybir.AluOpType

D = 2048
INW = 5144
FFN = 5632
NKC = 16
T = 512
EPS = 1e-6
O_FQ, O_FK, O_FV, O_FF, O_GQ, O_GK, O_GV, O_GG, O_GR, O_LG, O_LX = 0, 512, 1024, 1536, 1544, 1800, 2056, 2568, 3080, 3096, 4120
C_GMIX, C_GFFN, C_GFOX, C_GGLA, C_CW, C_CB, C_BA, C_BI, C_LAM, C_GLRU = 0, 16, 32, 40, 41, 73, 81, 89, 97, 105
NP = 113
NEG = -30000.0
SB_BASE = 16640
SB_END = 229376


class Sched:
    def __init__(self, nc, es):
        self.nc = nc
        self.es = es
        self.eng = {"pe": nc.tensor, "act": nc.scalar, "dve": nc.vector, "pool": nc.gpsimd, "sp": nc.sync}
        self.sem = {e: es.enter_context(nc.semaphore("sem_" + e)) for e in self.eng}
        self.cnt = {e: 0 for e in self.eng}
        self.waited = {e: {} for e in self.eng}
        self.lastw = {}
        self.readers = {}
        self.dmasem = {}
        self.dmacnt = {}
        self.dry = False

    def _wait(self, e, tok):
        if tok is None:
            return
        sem, val, src = tok
        if src == e and e == "pe":
            return
        w = self.waited[e]
        if w.get(id(sem), 0) >= val:
            return
        w[id(sem)] = val
        self.eng[e].wait_ge(sem, val)

    def _deps(self, e, reads, writes):
        for k in reads:
            self._wait(e, self.lastw.get(k))
        for k in writes:
            self._wait(e, self.lastw.get(k))
            for t in self.readers.get(k, {}).values():
                self._wait(e, t)

    def _record(self, tok, reads, writes):
        for k in reads:
            self.readers.setdefault(k, {})[id(tok[0])] = tok
        for k in writes:
            self.lastw[k] = tok
            self.readers[k] = {}

    def op(self, e, fn, reads=(), writes=()):
        if self.dry:
            return None
        self._deps(e, reads, writes)
        ins = fn(self.eng[e])
        self.cnt[e] += 1
        ins.then_inc(self.sem[e], 1)
        tok = (self.sem[e], self.cnt[e], e)
        self._record(tok, reads, writes)
        return tok

    def dma(self, q, pairs, semkey, reads=(), writes=()):
        if self.dry:
            return None
        self._deps(q, reads, writes)
        if semkey not in self.dmasem:
            self.dmasem[semkey] = self.es.enter_context(self.nc.semaphore("dsem_%d" % len(self.dmasem)))
            self.dmacnt[semkey] = 0
        sem = self.dmasem[semkey]
        for (o, i) in pairs:
            self.eng[q].dma_start(out=o, in_=i).then_inc(sem, 16)
            self.dmacnt[semkey] += 16
        tok = (sem, self.dmacnt[semkey], "dma")
        self._record(tok, reads, writes)
        return tok

    def barrier(self, engines=("pe", "act", "dve"), dma_keys=()):
        if self.dry:
            return
        toks = [(self.sem[e], self.cnt[e], e) for e in engines if self.cnt[e] > 0]
        for k in dma_keys:
            if k in self.dmasem:
                toks.append((self.dmasem[k], self.dmacnt[k], "dma"))
        for e in engines:
            for t in toks:
                if t[2] != e:
                    self._wait(e, t)

    def finish(self):
        for e in self.eng:
            if self.cnt[e] > 0 and e != "sp":
                self._wait("sp", (self.sem[e], self.cnt[e], e))
        for k, sem in self.dmasem.items():
            self._wait("sp", (sem, self.dmacnt[k], "dma"))


class WQ:
    def __init__(self, S, slots):
        self.S = S
        self.slots = slots
        self.plan = {c: [] for c in slots}
        self.reset()

    def reset(self):
        self.count = {c: 0 for c in self.slots}
        self.issued = {c: 0 for c in self.slots}

    def req(self, cls, fill):
        S = self.S
        slots = self.slots[cls]
        if S.dry:
            self.plan[cls].append(fill)
            return slots[0], (cls, 0)
        n = self.count[cls]
        depth = len(slots)
        plan = self.plan[cls]
        while self.issued[cls] < min(n + depth, len(plan)):
            j = self.issued[cls]
            s = j % depth
            S.dma("pool", plan[j](slots[s]), semkey=("w", cls, s), writes=[(cls, s)])
            self.issued[cls] += 1
        self.count[cls] += 1
        return slots[n % depth], (cls, n % depth)


def build_nc(NL, SEQ, dbg=False):
    NT = SEQ // T
    NBLK = SEQ // 128
    nc = bass.Bass("TRN2", target_bir_lowering=False)

    def din(name, shape):
        return nc.dram_tensor(name, list(shape), F32, kind="ExternalInput").ap()

    xT_in = din("xT", [D, SEQ])
    w_in = din("w_in", [NL, D, INW])
    w_out = din("w_out", [NL, D, D])
    w_gate = din("w_gate", [NL, D, FFN])
    w_up = din("w_up", [NL, D, FFN])
    w_down = din("w_down", [NL, FFN, D])
    p128 = din("p128", [NL, 128, NP])
    gfin = din("gfin", [128, 16])
    fbias = din("fbias", [NL, 8])
    gw2 = din("gw2", [NL, 16, 256])
    gbias = din("gbias", [NL, 256])
    lwa = din("lwa", [NL, 16, 64, 64])
    lwi = din("lwi", [NL, 16, 64, 64])
    c_tri = din("c_tri", [128, 128])
    c_triu = din("c_triu", [128, 128])
    c_ident = din("c_ident", [128, 128])
    c_mask = din("c_mask", [128, 896])
    outT = nc.dram_tensor("outT", [D, SEQ], F32, kind="ExternalOutput").ap()
    xs = nc.dram_tensor("xs", [D, SEQ], F32, kind="Internal").ap()
    dbg_out = {}
    if dbg:
        for nm, shp in [("d_hT", [128, 16 * 512]), ("d_fox", [128, 8 * 512]), ("d_gla", [128, 4 * 512]),
                        ("d_lru", [128, 8 * 512]), ("d_x1", [128, 16 * 512])]:
            dbg_out[nm] = nc.dram_tensor(nm, shp, F32, kind="ExternalOutput").ap()

    with contextlib.ExitStack() as es:
        S = Sched(nc, es)
        off = [SB_BASE]

        def sb(name, shape, dt):
            nb = int(np.prod(shape[1:])) * (4 if dt == F32 else 2)
            nb = (nb + 63) // 64 * 64
            assert off[0] + nb <= SB_END, ("SBUF overflow", name, off[0], nb)
            t = nc.alloc_sbuf_tensor_at(name, list(shape), dt, offset=off[0])
            off[0] += nb
            return t

        xT = sb("xTt", [128, 16, T], F32)
        hT = sb("hT", [128, 16, T], BF16)
        mix_fox = sb("mix_fox", [128, 8, T], BF16)
        mix_gla = sb("mix_gla", [128, 4, T], BF16)
        mix_lru = sb("mix_lru", [128, 8, T], BF16)
        kT = sb("kT", [128, 4, SEQ], BF16)
        vc = sb("vc", [128, NBLK, 8, 65], BF16)
        gk_tm = sb("gk_tm", [128, 4, 256], F32)
        gv_tm = sb("gv_tm", [128, 4, 512], BF16)
        call = sb("call", [128, NBLK, 8], F32)
        carry = sb("carry", [128, 8], F32)
        cref = sb("cref", [128, 8], F32)
        fb = sb("fb", [128, 8], F32)
        ones_bf = sb("ones_bf", [128, 128], BF16)
        ones_f = sb("ones_f", [128, 128], F32)
        tri_f = sb("tri_f", [128, 128], F32)
        triu_f = sb("triu_f", [128, 128], F32)
        ident_bf = sb("ident_bf", [128, 128], BF16)
        fmask = sb("fmask", [128, 896], BF16)
        P = sb("P", [128, NP], F32)
        GF = sb("GF", [128, 16], F32)
        w2 = sb("w2", [16, 256], BF16)
        gb = sb("gb", [1, 256], F32)
        Wa = sb("Wa", [128, 8, 128], BF16)
        Wi = sb("Wi", [128, 8, 128], BF16)
        cl = sb("cl", [128, 8], F32)
        S_f = sb("S_f", [128, 2, 128], F32)
        S_bf = sb("S_bf", [128, 2, 128], BF16)
        hst = sb("hst", [128, 8], F32)
        ctail = sb("ctail", [128, 8, 4], F32)
        rstd = sb("rstd", [128, T], F32)
        sq = [sb("sq%d" % i, [128, T], BF16) for i in range(2)]
        NA = 4
        WA = [sb("WA%d" % i, [128, 20, 128], BF16) for i in range(NA)]
        WB = [sb("WB%d" % i, [128, 16, 264], BF16) for i in range(2)]
        WD = [sb("WD%d" % i, [128, 22, 128], BF16) for i in range(2)]
        shared_base = off[0]
        off[0] = shared_base
        qT = [sb("qT%d" % i, [128, T], BF16) for i in range(2)]
        PT = [sb("PT%d" % i, [128, T], BF16) for i in range(3)]
        fox_o = sb("fox_o", [128, 8, T], F32)
        fbiasT = sb("fbiasT", [128, NBLK, 8], F32)
        rrow = sb("rrow", [128, T], F32)
        osb = sb("osb", [128, T], F32)
        ffraw = sb("ffraw", [128, 4, 8], F32)
        spf = sb("spf", [128, 4, 8], F32)
        end_fox = off[0]
        off[0] = shared_base
        grT = sb("grT", [128, T], BF16)
        spg = sb("spg", [128, 4, 256], F32)
        tmpE = sb("tmpE", [128, 1024], F32)
        Qt = sb("Qt", [128, 2, T], BF16)
        Kt = sb("Kt", [128, 2, T], BF16)
        Khat = sb("Khat", [128, 4, 256], BF16)
        eblast = sb("eblast", [128, 2, 4], F32)
        gsilu = sb("gsilu", [128, 4, T], BF16)
        gla_o = sb("gla_o", [128, 4, T], F32)
        ATb = [sb("AT%d" % i, [128, 128], BF16) for i in range(2)]
        end_gla = off[0]
        off[0] = shared_base
        lxbuf = [sb("lxbuf%d" % i, [128, 516], F32) for i in range(2)]
        xg = sb("xg", [128, T], F32)
        tA = sb("tA", [128, T], F32)
        tB = sb("tB", [128, T], F32)
        tC = sb("tC", [128, T], F32)
        tD = sb("tD", [128, T], F32)
        tE = sb("tE", [128, T], F32)
        glu = sb("glu", [128, T], F32)
        lin_bf = sb("lin_bf", [128, T], BF16)
        end_lru = off[0]
        off[0] = shared_base
        act = sb("act", [128, 22, T], BF16)
        sg = [sb("sg%d" % i, [128, T], F32) for i in range(2)]
        ost = [sb("ost%d" % i, [128, T], F32) for i in range(2)]
        end_ffn = off[0]

        ps = [es.enter_context(nc.psum_tensor("ps%d" % i, [128, 512], F32)) for i in range(8)]
        wq = WQ(S, {"A": WA, "B": WB, "D": WD})

        def mm(out, lhsT, rhs, start, stop, reads, writes):
            S.op("pe", lambda e: e.matmul(out, lhsT, rhs, start=start, stop=stop), reads=reads, writes=writes)

        def actf(out, in_, func, reads, writes, **kw):
            S.op("act", lambda e: e.activation(out=out, in_=in_, func=func, **kw), reads=reads, writes=writes)

        xs_v = xs.rearrange("(kc p) s -> p kc s", p=128)
        xin_v = xT_in.rearrange("(kc p) s -> p kc s", p=128)
        out_v = outT.rearrange("(kc p) s -> p kc s", p=128)
        XK = [("xT", k) for k in range(16)]
        HK = [("hT", k) for k in range(16)]

        def emit():
            wq.reset()
            rr = [0]

            def dbank():
                b = rr[0] % 4
                rr[0] += 1
                return b

            S.dma("sp", [(tri_f[:], c_tri[:, :]), (triu_f[:], c_triu[:, :]), (GF[:], gfin[:, :])], "cst", writes=["tri_f", "triu_f", "GF"])
            S.dma("pool", [(ident_bf[:], c_ident[:, :]), (fmask[:], c_mask[:, :])], "cst2", writes=["ident_bf", "fmask"])
            S.op("dve", lambda e: e.memset(ones_f[:], 1.0), writes=["ones_f"])
            S.op("dve", lambda e: e.memset(ones_bf[:], 1.0), writes=["ones_bf"])
            S.op("dve", lambda e: e.memset(vc[:, :, :, 64:65], 1.0), writes=["vc1"])
            S.op("dve", lambda e: e.memset(Wa[:], 0.0), writes=["Wa"])
            S.op("dve", lambda e: e.memset(Wi[:], 0.0), writes=["Wi"])

            def norm_to_hT(gt, gcol, Dn=D):
                for kc in range(16):
                    s_ = sq[kc % 2]
                    actf(s_[:], xT[:, kc, :], AF.Square, [("xT", kc)], [("sq", kc % 2)])
                    mm(ps[7][:, :], ones_bf[:], s_[:], kc == 0, kc == 15, [("sq", kc % 2), "ones_bf"], ["ps7"])
                actf(rstd[:], ps[7][:, :], AF.Ln, ["ps7"], ["rstd"], scale=1.0 / Dn, bias=EPS)
                actf(rstd[:], rstd[:], AF.Exp, ["rstd"], ["rstd"], scale=-0.5)

            def dense_fm(cols0, ncols, wsrc_v, evac):
                def fill(slot, c0=cols0, n=ncols):
                    return [(slot[:, 0:16, 0:n], wsrc_v[:, :, c0:c0 + n])]
                w, wk = wq.req("A", fill)
                b = dbank()
                bk = "ps%d" % b
                for kc in range(16):
                    mm(ps[b][0:ncols, :], w[:, kc, 0:ncols], hT[:, kc, :], kc == 0, kc == 15, [wk, ("hT", kc)], [bk])
                evac(ps[b], bk)

            for l in range(NL):
                S.dma("sp", [(P[:], p128[l]), (fb[:], fbias[l].partition_broadcast(128)), (gb[0:1, :], gbias[l:l + 1, :])],
                      "par", writes=["P", "fb", "gb"])
                lwa_v = lwa[l].rearrange("(c e) d f -> e d c f", e=2)
                lwi_v = lwi[l].rearrange("(c e) d f -> e d c f", e=2)
                S.dma("pool", [(w2[:], gw2[l]), (Wa[0:64, :, 0:64], lwa_v[0]), (Wa[64:128, :, 64:128], lwa_v[1]),
                               (Wi[0:64, :, 0:64], lwi_v[0]), (Wi[64:128, :, 64:128], lwi_v[1])], "par2", writes=["w2", "Wa", "Wi"])
                actf(cl[:], P[:, C_LAM:C_LAM + 8], AF.Exp, ["P"], ["cl"], scale=-1.0)
                actf(cl[:], cl[:], AF.Ln, ["cl"], ["cl"], bias=1.0)
                S.op("dve", lambda e: e.tensor_scalar(out=cl[:], in0=cl[:], scalar1=-8.0, scalar2=None, op0=ALU.mult), reads=["cl"], writes=["cl"])
                for tns, key in ((S_f, "S_f"), (S_bf, "S_bf"), (hst, "hst"), (ctail, "ctail"), (carry, "carry")):
                    S.op("dve", lambda e, tns=tns: e.memset(tns[:], 0.0), writes=[key])
                w_in_v = w_in[l].rearrange("(kc p) c -> p kc c", p=128)
                w_gate_v = w_gate[l].rearrange("(kc p) c -> p kc c", p=128)
                w_up_v = w_up[l].rearrange("(kc p) c -> p kc c", p=128)

                for t in range(NT):
                    t0 = t * T
                    nkb = 4 * (t + 1)
                    src_v = xin_v if l == 0 else xs_v
                    S.dma("sp", [(xT[:], src_v[:, :, t0:t0 + T])], "xld", reads=([("xs", t)] if l > 0 else []), writes=XK)
                    norm_to_hT(P, C_GMIX)
                    for kc in range(16):
                        S.op("dve", lambda e: e.scalar_tensor_tensor(out=hT[:, kc, :], in0=xT[:, kc, :], scalar=P[:, C_GMIX + kc:C_GMIX + kc + 1],
                                                                      in1=rstd[:], op0=ALU.mult, op1=ALU.mult),
                             reads=[("xT", kc), "P", "rstd"], writes=[("hT", kc)])
                    S.barrier()
                    slabs = [("fvA", 256, [(0, O_FV, 256)]), ("fvB", 256, [(0, O_FV + 256, 256)]),
                             ("ffgk", 264, [(0, O_FF, 8), (8, O_GK, 256)]),
                             ("gvA", 256, [(0, O_GV, 256)]), ("gvB", 256, [(0, O_GV + 256, 256)])]
                    for (nm, ncols, parts) in slabs:
                        def fill(slot, parts=parts, wv=w_in_v):
                            return [(slot[:, :, d0:d0 + n], wv[:, :, s0:s0 + n]) for (d0, s0, n) in parts]
                        w, wk = wq.req("B", fill)
                        for j in range(4):
                            b = dbank()
                            bk = "ps%d" % b
                            blk = 4 * t + j
                            for kc in range(16):
                                mm(ps[b][:, 0:ncols], hT[:, kc, j * 128:(j + 1) * 128], w[:, kc, 0:ncols], kc == 0, kc == 15, [wk, ("hT", kc)], [bk])
                            if nm in ("fvA", "fvB"):
                                h0 = 0 if nm == "fvA" else 4
                                actf(vc[:, blk, h0:h0 + 4, 0:64], ps[b][:, 0:256].rearrange("p (h d) -> p h d", d=64), AF.Copy, [bk], [("vc", blk)])
                            elif nm == "ffgk":
                                S.op("dve", lambda e: e.tensor_copy(out=ffraw[:, j, :], in_=ps[b][:, 0:8]), reads=[bk], writes=[("ffraw", j)])
                                actf(gk_tm[:, j, :], ps[b][:, 8:264], AF.Copy, [bk], [("gk_tm", j)])
                            else:
                                c0 = 0 if nm == "gvA" else 256
                                actf(gv_tm[:, j, c0:c0 + 256], ps[b][:, 0:256], AF.Copy, [bk], [("gv_tm", j)])
                    FR = [("ffraw", j) for j in range(4)]
                    for j in range(4):
                        S.op("dve", lambda e: e.tensor_tensor(out=spf[:, j, :], in0=ffraw[:, j, :], in1=fb[:], op=ALU.add), reads=[("ffraw", j), "fb"], writes=["spf"])
                    actf(spf[:], spf[:], AF.Exp, ["spf"], ["spf"], scale=-1.0)
                    actf(spf[:], spf[:], AF.Ln, ["spf"], ["spf"], bias=1.0)
                    for j in range(4):
                        for j2 in range(j):
                            mm(ps[7][:, j * 8:(j + 1) * 8], ones_f[:], spf[:, j2, :], j2 == 0, False, ["spf", "ones_f"], ["ps7"])
                        mm(ps[7][:, j * 8:(j + 1) * 8], tri_f[:], spf[:, j, :], j == 0, True, ["spf", "tri_f"], ["ps7"])
                    for j in range(4):
                        mm(ps[7][:, 32:40], ones_f[:], spf[:, j, :], j == 0, j == 3, ["spf", "ones_f"], ["ps7"])
                    for j in range(2):
                        mm(ps[7][:, 40:48], ones_f[:], spf[:, j, :], j == 0, j == 1, ["spf", "ones_f"], ["ps7"])
                    for j in range(4):
                        S.op("dve", lambda e: e.tensor_tensor(out=call[:, 4 * t + j, :], in0=carry[:], in1=ps[7][:, j * 8:(j + 1) * 8], op=ALU.subtract),
                             reads=["carry", "ps7"], writes=["call"])
                    S.op("dve", lambda e: e.tensor_tensor(out=cref[:], in0=carry[:], in1=ps[7][:, 40:48], op=ALU.subtract), reads=["carry", "ps7"], writes=["cref"])
                    S.op("dve", lambda e: e.tensor_tensor(out=carry[:], in0=carry[:], in1=ps[7][:, 32:40], op=ALU.subtract), reads=["carry", "ps7"], writes=["carry"])
                    for h in range(8):
                        S.op("dve", lambda e: e.tensor_scalar(out=fbiasT[:, 0:nkb, h], in0=call[:, 0:nkb, h], scalar1=cref[:, h:h + 1], scalar2=-1.0,
                                                               op0=ALU.subtract, op1=ALU.mult), reads=["call", "cref"], writes=["fbiasT"])
                    for hp in range(4):
                        q_ = qT[hp % 2]
                        qk_ = ("qT", hp % 2)
                        dense_fm(O_FQ + hp * 128, 128, w_in_v, lambda p_, bk: actf(q_[:], p_[:, :], AF.Copy, [bk], [qk_]))
                        dense_fm(O_FK + hp * 128, 128, w_in_v, lambda p_, bk: actf(kT[:, hp, t0:t0 + T], p_[:, :], AF.Copy, [bk], [("kT", hp)]))
                        for e_ in range(2):
                            h = 2 * hp + e_
                            lo = e_ * 64

                            def qk(kb):
                                sbk = 4 + kb % 2
                                key = "ps%d" % sbk
                                diag = kb >= 4 * t
                                mm(ps[sbk][:, :], kT[lo:lo + 64, hp, kb * 128:(kb + 1) * 128], q_[lo:lo + 64, :], True, not diag, [("kT", hp), qk_], [key])
                                if diag:
                                    r = kb - 4 * t
                                    mm(ps[sbk][:, :], ident_bf[:], fmask[:, 384 - r * 128:384 - r * 128 + 512], False, True, ["ident_bf", "fmask"], [key])
                                actf(PT[kb % 3][:], ps[sbk][:, :], AF.Exp, [key, "fbiasT"], [("PT", kb % 3)], scale=0.125, bias=fbiasT[:, kb, h:h + 1])

                            def pv(kb):
                                mm(ps[6][0:65, :], vc[:, kb, h, :], PT[kb % 3][:], kb == 0, kb == nkb - 1, [("vc", kb), "vc1", ("PT", kb % 3)], ["ps6"])

                            qk(0)
                            for kb in range(nkb):
                                if kb + 1 < nkb:
                                    qk(kb + 1)
                                pv(kb)
                            S.op("dve", lambda e: e.reciprocal(out=rrow[64:65, :], in_=ps[6][64:65, :]), reads=["ps6"], writes=["rrow"])
                            mm(ps[7][0:64, :], ones_f[64:65, 0:64], rrow[64:65, :], True, True, ["ones_f", "rrow"], ["ps7"])
                            actf(osb[0:64, :], ps[6][0:64, :], AF.Copy, ["ps6"], ["osb"])
                            S.op("dve", lambda e: e.tensor_tensor(out=fox_o[0:64, h, :], in0=osb[0:64, :], in1=ps[7][0:64, :], op=ALU.mult),
                                 reads=["osb", "ps7"], writes=[("fox_o", h)])
                    for h in range(8):
                        s_ = sq[h % 2]
                        actf(s_[0:64, :], fox_o[0:64, h, :], AF.Square, [("fox_o", h)], [("sq", h % 2)])
                        mm(ps[7][0:64, :], ones_bf[0:64, 0:64], s_[0:64, :], h == 0, h == 7, [("sq", h % 2), "ones_bf"], ["ps7"])
                    actf(rstd[0:64, :], ps[7][0:64, :], AF.Ln, ["ps7"], ["rstd"], scale=1.0 / 512, bias=EPS)
                    actf(rstd[0:64, :], rstd[0:64, :], AF.Exp, ["rstd"], ["rstd"], scale=-0.5)
                    for h in range(8):
                        S.op("dve", lambda e: e.scalar_tensor_tensor(out=mix_fox[0:64, h, :], in0=fox_o[0:64, h, :], scalar=P[0:64, C_GFOX + h:C_GFOX + h + 1],
                                                                      in1=rstd[0:64, :], op0=ALU.mult, op1=ALU.mult),
                             reads=[("fox_o", h), "P", "rstd"], writes=[("mix_fox", h)])
                    S.barrier()
                    dense_fm(O_GR, 16, w_in_v, lambda p_, bk: actf(grT[0:16, :], p_[0:16, :], AF.Copy, [bk], ["grT"]))
                    for j in range(4):
                        pb = 4 + j // 2
                        c0 = (j % 2) * 256
                        mm(ps[pb][:, c0:c0 + 256], grT[0:16, j * 128:(j + 1) * 128], w2[0:16, :], True, False, ["grT", "w2"], ["ps%d" % pb])
                        mm(ps[pb][:, c0:c0 + 256], ones_f[0:1, 0:128], gb[0:1, :], False, True, ["ones_f", "gb"], ["ps%d" % pb])
                    spg2 = spg[:].rearrange("p j c -> p (j c)")
                    for hf in range(2):
                        actf(spg2[:, hf * 512:(hf + 1) * 512], ps[4 + hf][:, :], AF.Exp, ["ps%d" % (4 + hf)], ["spg"], scale=-1.0)
                    actf(spg2, spg2, AF.Ln, ["spg"], ["spg"], bias=1.0)
                    for pr in range(2):
                        for j in range(4):
                            mm(ps[4 + pr][:, j * 128:(j + 1) * 128], spg[:, j, pr * 128:(pr + 1) * 128], tri_f[:], True, True, ["spg", "tri_f"], ["ps%d" % (4 + pr)])
                    for j in range(4):
                        pb = 6 + j // 2
                        c0 = (j % 2) * 256
                        mm(ps[pb][:, c0:c0 + 256], triu_f[:], spg[:, j, :], True, True, ["spg", "triu_f"], ["ps%d" % pb])
                    for pr in range(2):
                        actf(tmpE[:, pr * 512:(pr + 1) * 512], ps[4 + pr][:, :], AF.Exp, ["ps%d" % (4 + pr)], ["tmpE"], scale=-1.0 / 16)
                    for pr in range(2):
                        for j in range(4):
                            c_ = pr * 512 + j * 128 + 127
                            S.op("dve", lambda e: e.tensor_copy(out=eblast[:, pr, j:j + 1], in_=tmpE[:, c_:c_ + 1]), reads=["tmpE"], writes=["eblast"])
                    for pr in range(2):
                        dense_fm(O_GQ + pr * 128, 128, w_in_v, lambda p_, bk: S.op("dve", lambda e: e.scalar_tensor_tensor(
                            out=Qt[:, pr, :], in0=p_[:, :], scalar=0.125, in1=tmpE[:, pr * 512:(pr + 1) * 512], op0=ALU.mult, op1=ALU.mult),
                            reads=[bk, "tmpE"], writes=["Qt"]))
                    for pr in range(2):
                        actf(tmpE[:, pr * 512:(pr + 1) * 512], ps[4 + pr][:, :], AF.Exp, ["ps%d" % (4 + pr)], ["tmpE"], scale=1.0 / 16)
                    for pr in range(2):
                        dense_fm(O_GK + pr * 128, 128, w_in_v, lambda p_, bk: S.op("dve", lambda e: e.tensor_tensor(
                            out=Kt[:, pr, :], in0=p_[:, :], in1=tmpE[:, pr * 512:(pr + 1) * 512], op=ALU.mult), reads=[bk, "tmpE"], writes=["Kt"]))
                    for hf in range(2):
                        actf(tmpE[:, hf * 512:(hf + 1) * 512], ps[6 + hf][:, :], AF.Exp, ["ps%d" % (6 + hf)], ["tmpE"], scale=-1.0 / 16)
                    S.op("dve", lambda e: e.tensor_tensor(out=Khat[:].rearrange("p j c -> p (j c)"), in0=gk_tm[:].rearrange("p j c -> p (j c)"), in1=tmpE[:, :], op=ALU.mult),
                         reads=["tmpE"] + [("gk_tm", j) for j in range(4)], writes=["Khat"])
                    for h in range(4):
                        dense_fm(O_GG + h * 128, 128, w_in_v, lambda p_, bk: actf(gsilu[:, h, :], p_[:, :], AF.Silu, [bk], ["gsilu"]))
                    GV = [("gv_tm", j) for j in range(4)]
                    k_ = 0
                    for j in range(4):
                        js = slice(j * 128, (j + 1) * 128)
                        for pr in range(2):
                            for e_ in range(2):
                                h = 2 * pr + e_
                                lo = e_ * 64
                                a_ = ATb[k_ % 2]
                                ak = ("AT", k_ % 2)
                                c4 = (k_ % 4) * 128
                                k_ += 1
                                mm(ps[4][:, c4:c4 + 128], Kt[lo:lo + 64, pr, js], Qt[lo:lo + 64, pr, js], True, True, ["Kt", "Qt"], [("ps4", c4)])
                                S.op("dve", lambda e: e.tensor_tensor(out=a_[:], in0=ps[4][:, c4:c4 + 128], in1=tri_f[:], op=ALU.mult), reads=[("ps4", c4), "tri_f"], writes=[ak])
                                mm(ps[5][:, c4:c4 + 128], gv_tm[:, j, h * 128:(h + 1) * 128], a_[:], True, False, [("gv_tm", j), ak], [("ps5", c4)])
                                mm(ps[5][:, c4:c4 + 128], S_bf[lo:lo + 64, pr, :], Qt[lo:lo + 64, pr, js], False, True, ["S_bf", "Qt"], [("ps5", c4)])
                                actf(gla_o[:, h, js], ps[5][:, c4:c4 + 128], AF.Copy, [("ps5", c4)], [("gla_o", h)])
                                mm(ps[6][lo:lo + 64, pr * 128:(pr + 1) * 128], Khat[:, j, h * 64:(h + 1) * 64], gv_tm[:, j, h * 128:(h + 1) * 128], True, True,
                                   ["Khat", ("gv_tm", j)], [("ps6", pr)])
                            S.op("dve", lambda e: e.scalar_tensor_tensor(out=S_f[:, pr, :], in0=S_f[:, pr, :], scalar=eblast[:, pr, j:j + 1],
                                                                          in1=ps[6][:, pr * 128:(pr + 1) * 128], op0=ALU.mult, op1=ALU.add),
                                 reads=["S_f", "eblast", ("ps6", pr)], writes=["S_f"])
                            actf(S_bf[:, pr, :], S_f[:, pr, :], AF.Copy, ["S_f"], ["S_bf"])
                    for h in range(4):
                        s_ = sq[h % 2]
                        actf(s_[:], gla_o[:, h, :], AF.Square, [("gla_o", h)], [("sq", h % 2)])
                        mm(ps[7][:, :], ones_bf[:], s_[:], True, True, [("sq", h % 2), "ones_bf"], ["ps7"])
                        actf(rstd[:], ps[7][:, :], AF.Ln, ["ps7"], ["rstd"], scale=1.0 / 128, bias=EPS)
                        actf(rstd[:], rstd[:], AF.Exp, ["rstd"], ["rstd"], scale=-0.5)
                        S.op("dve", lambda e: e.scalar_tensor_tensor(out=gla_o[:, h, :], in0=gla_o[:, h, :], scalar=P[:, C_GGLA:C_GGLA + 1], in1=rstd[:],
                                                                      op0=ALU.mult, op1=ALU.mult), reads=[("gla_o", h), "P", "rstd"], writes=[("gla_o", h)])
                        S.op("dve", lambda e: e.tensor_tensor(out=mix_gla[:, h, :], in0=gla_o[:, h, :], in1=gsilu[:, h, :], op=ALU.mult),
                             reads=[("gla_o", h), "gsilu"], writes=[("mix_gla", h)])
                    S.barrier()
                    for c in range(8):
                        lb = lxbuf[c % 2]
                        lk = ("lxbuf", c % 2)
                        S.op("dve", lambda e: e.tensor_copy(out=lb[:, 0:3], in_=ctail[:, c, 0:3]), reads=["ctail"], writes=[lk])
                        dense_fm(O_LX + c * 128, 128, w_in_v, lambda p_, bk: actf(lb[:, 3:515], p_[:, :], AF.Copy, [bk], [lk]))
                        S.op("dve", lambda e: e.tensor_copy(out=ctail[:, c, 0:3], in_=lb[:, 512:515]), reads=[lk], writes=["ctail"])
                        dense_fm(O_LG + c * 128, 128, w_in_v, lambda p_, bk: actf(xg[:], p_[:, :], AF.Copy, [bk], ["xg"]))
                        actf(tA[:], xg[:], AF.Square, ["xg"], ["tA"])
                        S.op("dve", lambda e: e.tensor_scalar(out=tA[:], in0=tA[:], scalar1=0.044715, scalar2=1.0, op0=ALU.mult, op1=ALU.add), reads=["tA"], writes=["tA"])
                        S.op("dve", lambda e: e.tensor_tensor(out=tA[:], in0=tA[:], in1=xg[:], op=ALU.mult), reads=["tA", "xg"], writes=["tA"])
                        actf(tA[:], tA[:], AF.Sigmoid, ["tA"], ["tA"], scale=1.5957691216057308)
                        S.op("dve", lambda e: e.tensor_tensor(out=glu[:], in0=tA[:], in1=xg[:], op=ALU.mult), reads=["tA", "xg"], writes=["glu"])
                        cw = C_CW + c * 4
                        S.op("dve", lambda e: e.tensor_scalar(out=tB[:], in0=lb[:, 0:512], scalar1=P[:, cw:cw + 1], scalar2=P[:, C_CB + c:C_CB + c + 1],
                                                               op0=ALU.mult, op1=ALU.add), reads=[lk, "P"], writes=["tB"])
                        for jj in range(1, 4):
                            S.op("dve", lambda e: e.scalar_tensor_tensor(out=tB[:], in0=lb[:, jj:jj + 512], scalar=P[:, cw + jj:cw + jj + 1], in1=tB[:],
                                                                          op0=ALU.mult, op1=ALU.add), reads=[lk, "P", "tB"], writes=["tB"])
                        actf(lin_bf[:], tB[:], AF.Copy, ["tB"], ["lin_bf"])
                        mm(ps[4][:, :], Wa[:, c, :], lin_bf[:], True, True, ["Wa", "lin_bf"], ["ps4"])
                        mm(ps[5][:, :], Wi[:, c, :], lin_bf[:], True, True, ["Wi", "lin_bf"], ["ps5"])
                        actf(tC[:], ps[4][:, :], AF.Sigmoid, ["ps4", "P"], ["tC"], bias=P[:, C_BA + c:C_BA + c + 1])
                        actf(tD[:], ps[5][:, :], AF.Sigmoid, ["ps5", "P"], ["tD"], bias=P[:, C_BI + c:C_BI + c + 1])
                        actf(tC[:], tC[:], AF.Exp, ["tC", "cl"], ["tC"], scale=cl[:, c:c + 1])
                        actf(tE[:], tC[:], AF.Square, ["tC"], ["tE"])
                        actf(tE[:], tE[:], AF.Sqrt, ["tE"], ["tE"], scale=-1.0, bias=1.0)
                        S.op("dve", lambda e: e.tensor_tensor(out=tD[:], in0=tD[:], in1=tE[:], op=ALU.mult), reads=["tD", "tE"], writes=["tD"])
                        S.op("dve", lambda e: e.tensor_tensor(out=tD[:], in0=tD[:], in1=tB[:], op=ALU.mult), reads=["tD", "tB"], writes=["tD"])
                        S.op("dve", lambda e: e.tensor_tensor_scan(out=tE[:], data0=tC[:], data1=tD[:], initial=hst[:, c:c + 1], op0=ALU.mult, op1=ALU.add),
                             reads=["tC", "tD", "hst"], writes=["tE"])
                        S.op("dve", lambda e: e.tensor_copy(out=hst[:, c:c + 1], in_=tE[:, 511:512]), reads=["tE"], writes=["hst"])
                        S.op("dve", lambda e: e.tensor_tensor(out=tE[:], in0=tE[:], in1=glu[:], op=ALU.mult), reads=["tE", "glu"], writes=["tE"])
                        actf(mix_lru[:, c, :], tE[:], AF.Copy, ["tE"], [("mix_lru", c)])
                        s_ = sq[c % 2]
                        actf(s_[:], tE[:], AF.Square, ["tE"], [("sq", c % 2)])
                        mm(ps[7][:, :], ones_bf[:], s_[:], c == 0, c == 7, [("sq", c % 2), "ones_bf"], ["ps7"])
                    actf(rstd[:], ps[7][:, :], AF.Ln, ["ps7"], ["rstd"], scale=1.0 / 1024, bias=EPS)
                    actf(rstd[:], rstd[:], AF.Exp, ["rstd"], ["rstd"], scale=-0.5)
                    for c in range(8):
                        S.op("dve", lambda e: e.scalar_tensor_tensor(out=mix_lru[:, c, :], in0=mix_lru[:, c, :], scalar=P[:, C_GLRU + c:C_GLRU + c + 1], in1=rstd[:],
                                                                      op0=ALU.mult, op1=ALU.mult), reads=[("mix_lru", c), "P", "rstd"], writes=[("mix_lru", c)])
                    S.barrier()
                    if dbg and l == 0 and t == NT - 1:
                        def dump(nm, src, n):
                            for i in range(n):
                                actf(tA[:], src(i), AF.Copy, ["dbgsrc"], ["tA"])
                                S.dma("sp", [(dbg_out[nm][:, i * 512:(i + 1) * 512], tA[:])], "dbg", reads=["tA"])
                        dump("d_fox", lambda i: mix_fox[:, i, :], 8)
                        dump("d_gla", lambda i: mix_gla[:, i, :], 4)
                        dump("d_lru", lambda i: mix_lru[:, i, :], 8)
                        S.barrier()
                    MK = [("mix_fox", h) for h in range(8)] + [("mix_gla", h) for h in range(4)] + [("mix_lru", c) for c in range(8)]
                    for oc in range(16):
                        cs = slice(oc * 128, (oc + 1) * 128)

                        def fill(slot, cs=cs, l=l):
                            return [(slot[0:64, 0:8, :], w_out[l][0:512, cs].rearrange("(h d) c -> d h c", d=64)),
                                    (slot[:, 8:12, :], w_out[l][512:1024, cs].rearrange("(h p) c -> p h c", p=128)),
                                    (slot[:, 12:20, :], w_out[l][1024:2048, cs].rearrange("(c p) n -> p c n", p=128))]
                        w, wk = wq.req("A", fill)
                        b = dbank()
                        bk = "ps%d" % b
                        for h in range(8):
                            mm(ps[b][:, :], w[0:64, h, :], mix_fox[0:64, h, :], h == 0, False, [wk, ("mix_fox", h)], [bk])
                        for h in range(4):
                            mm(ps[b][:, :], w[:, 8 + h, :], mix_gla[:, h, :], False, False, [wk, ("mix_gla", h)], [bk])
                        for c in range(8):
                            mm(ps[b][:, :], w[:, 12 + c, :], mix_lru[:, c, :], False, c == 7, [wk, ("mix_lru", c)], [bk])
                        S.op("dve", lambda e: e.tensor_tensor(out=xT[:, oc, :], in0=ps[b][:, :], in1=xT[:, oc, :], op=ALU.add), reads=[bk, ("xT", oc)], writes=[("xT", oc)])
                    if dbg and l == 0 and t == NT - 1:
                        S.dma("sp", [(dbg_out["d_x1"].rearrange("p (k s) -> p k s", k=16), xT[:])], "dbg2", reads=XK)
                    norm_to_hT(P, C_GFFN)
                    for kc in range(16):
                        S.op("dve", lambda e: e.scalar_tensor_tensor(out=hT[:, kc, :], in0=xT[:, kc, :], scalar=P[:, C_GFFN + kc:C_GFFN + kc + 1],
                                                                      in1=rstd[:], op0=ALU.mult, op1=ALU.mult),
                             reads=[("xT", kc), "P", "rstd"], writes=[("hT", kc)])
                    for hh in range(2):
                        for hl in range(22):
                            hc = hh * 22 + hl
                            res = {}
                            for nm, wv in (("g", w_gate_v), ("u", w_up_v)):
                                def fill(slot, wv=wv, hc=hc):
                                    return [(slot[:, 0:16, :], wv[:, :, hc * 128:(hc + 1) * 128])]
                                w, wk = wq.req("A", fill)
                                b = dbank()
                                for kc in range(16):
                                    mm(ps[b][:, :], w[:, kc, :], hT[:, kc, :], kc == 0, kc == 15, [wk, ("hT", kc)], ["ps%d" % b])
                                res[nm] = b
                            s_ = sg[hl % 2]
                            actf(s_[:], ps[res["g"]][:, :], AF.Silu, ["ps%d" % res["g"]], [("sg", hl % 2)])
                            S.op("dve", lambda e: e.tensor_tensor(out=act[:, hl, :], in0=s_[:], in1=ps[res["u"]][:, :], op=ALU.mult),
                                 reads=[("sg", hl % 2), "ps%d" % res["u"]], writes=[("act", hl)])
                        for oc in range(16):
                            def fill(slot, hh=hh, oc=oc, l=l):
                                return [(slot[:, 0:22, :], w_down[l][hh * 2816:(hh + 1) * 2816, oc * 128:(oc + 1) * 128].rearrange("(c p) n -> p c n", p=128))]
                            w, wk = wq.req("D", fill)
                            b = dbank()
                            bk = "ps%d" % b
                            for hl in range(22):
                                mm(ps[b][:, :], w[:, hl, :], act[:, hl, :], hl == 0, hl == 21, [wk, ("act", hl)], [bk])
                            S.op("dve", lambda e: e.tensor_tensor(out=xT[:, oc, :], in0=ps[b][:, :], in1=xT[:, oc, :], op=ALU.add), reads=[bk, ("xT", oc)], writes=[("xT", oc)])
                    if l < NL - 1:
                        S.dma("sp", [(xs_v[:, :, t0:t0 + T], xT[:])], "xst", reads=XK, writes=[("xs", t)])
                    else:
                        norm_to_hT(GF, 0)
                        for kc in range(16):
                            o_ = ost[kc % 2]
                            S.op("dve", lambda e: e.scalar_tensor_tensor(out=o_[:], in0=xT[:, kc, :], scalar=GF[:, kc:kc + 1], in1=rstd[:], op0=ALU.mult, op1=ALU.mult),
                                 reads=[("xT", kc), "GF", "rstd"], writes=[("ost", kc % 2)])
                            S.dma("sp", [(outT[kc * 128:(kc + 1) * 128, t0:t0 + T], o_[:])], ("ost", kc % 2), reads=[("ost", kc % 2)])
                    S.barrier(dma_keys=[("ost", 0), ("ost", 1), "dbg", "dbg2"])

        S.dry = True
        emit()
        S.dry = False
        emit()
        S.finish()
    return nc


def host_inputs(inputs, NL, SEQ):
    f = lambda a: np.ascontiguousarray(np.asarray(a, dtype=np.float32))
    p = np.zeros((NL, 128, NP), np.float32)
    p[:, :, C_GMIX:C_GMIX + 16] = f(inputs["norm_mix"])[:NL].reshape(NL, 16, 128).transpose(0, 2, 1)
    p[:, :, C_GFFN:C_GFFN + 16] = f(inputs["norm_ffn"])[:NL].reshape(NL, 16, 128).transpose(0, 2, 1)
    p[:, 0:64, C_GFOX:C_GFOX + 8] = f(inputs["fox_out_norm"])[:NL].reshape(NL, 8, 64).transpose(0, 2, 1)
    p[:, :, C_GGLA] = f(inputs["gla_head_norm"])[:NL]
    p[:, :, C_CW:C_CW + 32] = f(inputs["conv_w"])[:NL].reshape(NL, 4, 8, 128).transpose(0, 3, 2, 1).reshape(NL, 128, 32)
    for nm, c in (("conv_b", C_CB), ("lru_b_a", C_BA), ("lru_b_i", C_BI), ("lru_lambda", C_LAM), ("lru_out_norm", C_GLRU)):
        p[:, :, c:c + 8] = f(inputs[nm])[:NL].reshape(NL, 8, 128).transpose(0, 2, 1)
    k = np.arange(128)
    tri = (k[:, None] <= k[None, :]).astype(np.float32)
    triu = (k[:, None] > k[None, :]).astype(np.float32)
    jj = np.arange(896)
    mask = np.where((jj[None, :] - 384) >= k[:, None], 0.0, NEG).astype(np.float32)
    common = {
        "w_in": f(inputs["w_in"])[:NL], "w_out": f(inputs["w_out"])[:NL], "w_gate": f(inputs["w_gate"])[:NL],
        "w_up": f(inputs["w_up"])[:NL], "w_down": f(inputs["w_down"])[:NL], "p128": p,
        "gfin": np.ascontiguousarray(f(inputs["final_norm"]).reshape(16, 128).T),
        "fbias": f(inputs["fox_f_bias"])[:NL], "gw2": f(inputs["gla_gate_w2"])[:NL], "gbias": f(inputs["gla_gate_bias"])[:NL],
        "lwa": f(inputs["lru_w_a"])[:NL], "lwi": f(inputs["lru_w_i"])[:NL],
        "c_tri": tri, "c_triu": triu, "c_ident": np.eye(128, dtype=np.float32), "c_mask": mask,
    }
    return common


_NC_CACHE = {}


def kernel(**inputs):
    x = np.asarray(inputs["x"], dtype=np.float32)
    B, SEQ, _ = x.shape
    NL = int(np.asarray(inputs["w_in"]).shape[0])
    common = host_inputs(inputs, NL, SEQ)
    key = (NL, SEQ)
    if key not in _NC_CACHE:
        _NC_CACHE[key] = build_nc(NL, SEQ)
    nc = _NC_CACHE[key]
    in_maps = []
    for b in range(B):
        m = dict(common)
        m["xT"] = np.ascontiguousarray(x[b].T)
        in_maps.append(m)
    res = run_bass_kernel_spmd(nc, in_maps, core_ids=list(range(B)))
    out = np.stack([np.ascontiguousarray(np.asarray(r["outT"], dtype=np.float32).T) for r in res.results], axis=0)
    return out.astype(np.float32)
```

```python
import contextlib
import numpy as np
import concourse.bass as bass
import concourse.mybir as mybir
from concourse.bass_utils import run_bass_kernel_spmd

F32 = mybir.dt.float32
BF16 = mybir.dt.bfloat16
AF = mybir.ActivationFunctionType
ALU = mybir.AluOpType

D = 2048
INW = 5144
FFN = 5632
NKC = 16
T = 512
EPS = 1e-6
O_FQ, O_FK, O_FV, O_FF, O_GQ, O_GK, O_GV, O_GG, O_GR, O_LG, O_LX = 0, 512, 1024, 1536, 1544, 1800, 2056, 2568, 3080, 3096, 4120
C_GMIX, C_GFFN, C_GFOX, C_GGLA, C_CW, C_CB, C_BA, C_BI, C_LAM, C_GLRU = 0, 16, 32, 40, 41, 73, 81, 89, 97, 105
NP = 113
NEG = -30000.0
FM_COLS = []
for _hp in range(4):
    FM_COLS += [O_FQ + _hp * 128, O_FK + _hp * 128]
FM_COLS += [O_GR] + [O_GQ, O_GQ + 128, O_GK, O_GK + 128] + [O_GG + _h * 128 for _h in range(4)]
for _c in range(8):
    FM_COLS += [O_LX + _c * 128, O_LG + _c * 128]
FM_IDX = {c: i for i, c in enumerate(FM_COLS)}
NFM = len(FM_COLS)
TM_SLABS = [[(0, O_FV, 256)], [(0, O_FV + 256, 256)], [(0, O_FF, 8), (8, O_GK, 256)], [(0, O_GV, 256)], [(0, O_GV + 256, 256)]]
SB_BASE = 16640
SB_END = 229376


class Sched:
    def __init__(self, nc, es):
        self.nc = nc
        self.es = es
        self.eng = {"pe": nc.tensor, "act": nc.scalar, "dve": nc.vector, "pool": nc.gpsimd, "sp": nc.sync}
        self.sem = {e: es.enter_context(nc.semaphore("sem_" + e)) for e in self.eng}
        self.cnt = {e: 0 for e in self.eng}
        self.waited = {e: {} for e in self.eng}
        self.lastw = {}
        self.readers = {}
        self.dmasem = {}
        self.dmacnt = {}
        self.dry = False

    def _wait(self, e, tok):
        if tok is None:
            return
        sem, val, src = tok
        if src == e and e == "pe":
            return
        w = self.waited[e]
        if w.get(id(sem), 0) >= val:
            return
        w[id(sem)] = val
        self.eng[e].wait_ge(sem, val)

    def _deps(self, e, reads, writes):
        for k in reads:
            self._wait(e, self.lastw.get(k))
        for k in writes:
            self._wait(e, self.lastw.get(k))
            for t in self.readers.get(k, {}).values():
                self._wait(e, t)

    def _record(self, tok, reads, writes):
        for k in reads:
            self.readers.setdefault(k, {})[id(tok[0])] = tok
        for k in writes:
            self.lastw[k] = tok
            self.readers[k] = {}

    def op(self, e, fn, reads=(), writes=()):
        if self.dry:
            return None
        self._deps(e, reads, writes)
        ins = fn(self.eng[e])
        self.cnt[e] += 1
        ins.then_inc(self.sem[e], 1)
        tok = (self.sem[e], self.cnt[e], e)
        self._record(tok, reads, writes)
        return tok

    def dma(self, q, pairs, semkey, reads=(), writes=()):
        if self.dry:
            return None
        self._deps(q, reads, writes)
        if semkey not in self.dmasem:
            self.dmasem[semkey] = self.es.enter_context(self.nc.semaphore("dsem_%d" % len(self.dmasem)))
            self.dmacnt[semkey] = 0
        sem = self.dmasem[semkey]
        for (o, i) in pairs:
            if q == "pool":
                self.eng[q].dma_start(out=o, in_=i, max_dma_last_dim=8192).then_inc(sem, 16)
            else:
                self.eng[q].dma_start(out=o, in_=i).then_inc(sem, 16)
            self.dmacnt[semkey] += 16
        tok = (sem, self.dmacnt[semkey], "dma")
        self._record(tok, reads, writes)
        return tok

    def barrier(self, engines=("pe", "act", "dve"), dma_keys=()):
        if self.dry:
            return
        toks = [(self.sem[e], self.cnt[e], e) for e in engines if self.cnt[e] > 0]
        for k in dma_keys:
            if k in self.dmasem:
                toks.append((self.dmasem[k], self.dmacnt[k], "dma"))
        for e in engines:
            for t in toks:
                if t[2] != e:
                    self._wait(e, t)

    def finish(self):
        for e in self.eng:
            if self.cnt[e] > 0 and e != "sp":
                self._wait("sp", (self.sem[e], self.cnt[e], e))
        for k, sem in self.dmasem.items():
            self._wait("sp", (sem, self.dmacnt[k], "dma"))


class WQ:
    def __init__(self, S, slots):
        self.S = S
        self.slots = slots
        self.plan = {c: [] for c in slots}
        self.reset()

    def reset(self):
        self.count = {c: 0 for c in self.slots}
        self.issued = {c: 0 for c in self.slots}

    def req(self, cls, fill):
        S = self.S
        slots = self.slots[cls]
        if S.dry:
            self.plan[cls].append(fill)
            return slots[0], (cls, 0)
        n = self.count[cls]
        depth = len(slots)
        plan = self.plan[cls]
        while self.issued[cls] < min(n + depth, len(plan)):
            j = self.issued[cls]
            s = j % depth
            S.dma("pool", plan[j](slots[s]), semkey=("w", cls, s), writes=[(cls, s)])
            self.issued[cls] += 1
        self.count[cls] += 1
        return slots[n % depth], (cls, n % depth)


def build_nc(NL, SEQ, dbg=False, ncores=8):
    NT = SEQ // T
    NB = SEQ // 128
    NBLK = 2 * NB
    nc = bass.Bass("TRN2", target_bir_lowering=False)

    def din(name, shape):
        return nc.dram_tensor(name, list(shape), F32, kind="ExternalInput").ap()

    xT_in = din("xT", [D, SEQ])
    w_in_fm = din("w_in_fm", [NL, NFM, 128, 16, 128])
    w_in_tm = din("w_in_tm", [NL, 5, 128, 16, 264])
    w_out_r = din("w_out_r", [NL, 16, 128, 20, 128])
    w_gate_r = din("w_gate_r", [NL, 44, 128, 16, 128])
    w_up_r = din("w_up_r", [NL, 44, 128, 16, 128])
    w_down_r = din("w_down_r", [NL, 2, 16, 128, 22, 128])
    p128 = din("p128", [NL, 128, NP])
    gfin = din("gfin", [128, 16])
    fbias = din("fbias", [NL, 8])
    gw2 = din("gw2", [NL, 16, 256])
    gbias = din("gbias", [NL, 256])
    lwa = din("lwa", [NL, 16, 64, 64])
    lwi = din("lwi", [NL, 16, 64, 64])
    c_tri = din("c_tri", [128, 128])
    c_triu = din("c_triu", [128, 128])
    c_ident = din("c_ident", [128, 128])
    c_mask = din("c_mask", [128, 896])
    pflag = din("pflag", [128, 2])
    NSB = 4 * SEQ + NB * 8 * 65
    NSF = NB * 8 + 8 + 256 + 8 + 32
    NSV = NB * 8 * 65
    snd_b = nc.dram_tensor("snd_b", [128, 4 * SEQ], BF16)
    rcv_b = nc.dram_tensor("rcv_b", [256, 4 * SEQ], BF16)
    snd_v = nc.dram_tensor("snd_v", [128, NSV], BF16)
    rcv_v = nc.dram_tensor("rcv_v", [256, NSV], BF16)
    snd_f = nc.dram_tensor("snd_f", [128, NSF], F32)
    rcv_f = nc.dram_tensor("rcv_f", [256, NSF], F32)
    outT = nc.dram_tensor("outT", [D, SEQ], F32, kind="ExternalOutput").ap()
    xs = nc.dram_tensor("xs", [D, SEQ], F32, kind="Internal").ap()
    dbg_out = {}
    if dbg:
        for nm, shp in [("d_hT", [128, 16 * 512]), ("d_fox", [128, 8 * 512]), ("d_gla", [128, 4 * 512]),
                        ("d_lru", [128, 8 * 512]), ("d_x1", [128, 16 * 512])]:
            dbg_out[nm] = nc.dram_tensor(nm, shp, F32, kind="ExternalOutput").ap()

    with contextlib.ExitStack() as es:
        S = Sched(nc, es)
        off = [SB_BASE]

        def sb(name, shape, dt):
            nb = int(np.prod(shape[1:])) * (4 if dt == F32 else 2)
            nb = (nb + 63) // 64 * 64
            assert off[0] + nb <= SB_END, ("SBUF overflow", name, off[0], nb)
            t = nc.alloc_sbuf_tensor_at(name, list(shape), dt, offset=off[0])
            off[0] += nb
            return t

        xT = sb("xTt", [128, 16, T], F32)
        hT = sb("hT", [128, 16, T], BF16)
        mix_fox = sb("mix_fox", [128, 8, T], BF16)
        mix_gla = sb("mix_gla", [128, 4, T], BF16)
        mix_lru = sb("mix_lru", [128, 8, T], BF16)
        kT = sb("kT", [128, 4, 2 * SEQ], BF16)
        vc = sb("vc", [128, NBLK, 8, 65], BF16)
        gk_tm = sb("gk_tm", [128, 4, 256], F32)
        gv_tm = sb("gv_tm", [128, 4, 512], BF16)
        call = sb("call", [128, NBLK, 8], F32)
        carry = sb("carry", [128, 8], F32)
        cref = sb("cref", [128, 8], F32)
        fb = sb("fb", [128, 8], F32)
        pf = sb("pf", [128, 2], F32)
        ones_bf = sb("ones_bf", [128, 128], BF16)
        ones_f = sb("ones_f", [128, 128], F32)
        tri_f = sb("tri_f", [128, 128], F32)
        triu_f = sb("triu_f", [128, 128], F32)
        ident_bf = sb("ident_bf", [128, 128], BF16)
        fmask = sb("fmask", [128, 896], BF16)
        P = sb("P", [128, NP], F32)
        GF = sb("GF", [128, 16], F32)
        w2 = sb("w2", [16, 256], BF16)
        gb = sb("gb", [1, 256], F32)
        Wa = sb("Wa", [128, 8, 128], BF16)
        Wi = sb("Wi", [128, 8, 128], BF16)
        cl = sb("cl", [128, 8], F32)
        S_f = sb("S_f", [128, 2, 128], F32)
        S_bf = sb("S_bf", [128, 2, 128], BF16)
        hst = sb("hst", [128, 8], F32)
        ctail = sb("ctail", [128, 8, 4], F32)
        rstd = sb("rstd", [128, T], F32)
        sq = [sb("sq%d" % i, [128, T], BF16) for i in range(2)]
        NA = 4
        WA = [sb("WA%d" % i, [128, 20, 128], BF16) for i in range(NA)]
        WB = [sb("WB%d" % i, [128, 16, 264], BF16) for i in range(2)]
        WD = [sb("WD%d" % i, [128, 22, 128], BF16) for i in range(2)]
        shared_base = off[0]
        off[0] = shared_base
        qT = [sb("qT%d" % i, [128, T], BF16) for i in range(2)]
        PT = [sb("PT%d" % i, [128, T], BF16) for i in range(3)]
        fbiasT = sb("fbiasT", [128, NBLK, 8], F32)
        osb = sb("osb", [128, T], F32)
        rrow = osb
        ffraw = sb("ffraw", [128, 4, 8], F32)
        spf = sb("spf", [128, 4, 8], F32)
        end_fox = off[0]
        lxbuf = [sb("lxbuf%d" % i, [128, 516], F32) for i in range(2)]
        xg = sb("xg", [128, T], F32)
        tA = sb("tA", [128, T], F32)
        tB = sb("tB", [128, T], F32)
        tC = sb("tC", [128, T], F32)
        tD = sb("tD", [128, T], F32)
        tE = sb("tE", [128, T], F32)
        glu = sb("glu", [128, T], F32)
        lin_bf = sb("lin_bf", [128, T], BF16)
        lssq = sb("lssq", [128, T], F32)
        lsq = sb("lsq", [128, T], BF16)
        end_lru = off[0]
        off[0] = shared_base
        grT = sb("grT", [128, T], BF16)
        spg = sb("spg", [128, 4, 256], F32)
        tmpE = sb("tmpE", [128, 1024], F32)
        Qt = sb("Qt", [128, 2, T], BF16)
        Kt = sb("Kt", [128, 2, T], BF16)
        Khat = sb("Khat", [128, 4, 256], BF16)
        eblast = sb("eblast", [128, 2, 4], F32)
        gsilu = sb("gsilu", [128, 4, T], BF16)
        gla_o = sb("gla_o", [128, 4, T], F32)
        ATb = [sb("AT%d" % i, [128, 128], BF16) for i in range(2)]
        end_gla = off[0]
        off[0] = shared_base
        act = sb("act", [128, 22, T], BF16)
        sg = [sb("sg%d" % i, [128, T], F32) for i in range(2)]
        ost = [sb("ost%d" % i, [128, T], F32) for i in range(2)]
        end_ffn = off[0]

        ps = [es.enter_context(nc.psum_tensor("ps%d" % i, [128, 512], F32)) for i in range(8)]
        wq = WQ(S, {"A": WA, "B": WB, "D": WD})

        def mm(out, lhsT, rhs, start, stop, reads, writes):
            S.op("pe", lambda e: e.matmul(out, lhsT, rhs, start=start, stop=stop), reads=reads, writes=writes)

        def actf(out, in_, func, reads, writes, **kw):
            S.op("act", lambda e: e.activation(out=out, in_=in_, func=func, **kw), reads=reads, writes=writes)

        xs_v = xs.rearrange("(kc p) s -> p kc s", p=128)
        xin_v = xT_in.rearrange("(kc p) s -> p kc s", p=128)
        out_v = outT.rearrange("(kc p) s -> p kc s", p=128)
        XK = [("xT", k) for k in range(16)]
        HK = [("hT", k) for k in range(16)]

        cc_sem = es.enter_context(nc.semaphore("cc_sem"))
        cc_cnt = [0]

        def emit():
            wq.reset()
            rr = [0]

            DB = [[0, 1, 2, 3]]

            def dbank():
                b = DB[0][rr[0] % len(DB[0])]
                rr[0] += 1
                return b

            S.dma("sp", [(tri_f[:], c_tri[:, :]), (triu_f[:], c_triu[:, :]), (GF[:], gfin[:, :])], "cst", writes=["tri_f", "triu_f", "GF"])
            S.dma("pool", [(ident_bf[:], c_ident[:, :]), (fmask[:], c_mask[:, :])], "cst2", writes=["ident_bf", "fmask"])
            S.op("dve", lambda e: e.memset(ones_f[:], 1.0), writes=["ones_f"])
            S.op("dve", lambda e: e.memset(ones_bf[:], 1.0), writes=["ones_bf"])
            S.dma("sp", [(pf[:], pflag[:, :])], "cst3", writes=["pf"])
            S.op("dve", lambda e: e.memset(vc[:], 0.0), writes=["vc1"] + [("vc", b_) for b_ in range(NBLK)])
            S.op("dve", lambda e: e.memset(vc[:, :, :, 64:65], 1.0), writes=["vc1"])
            S.op("dve", lambda e: e.memset(kT[:], 0.0), writes=[("kT", h_) for h_ in range(4)])
            S.op("dve", lambda e: e.memset(call[:], 0.0), writes=["call"])
            for tns, key in ((S_f, "S_f"), (S_bf, "S_bf"), (hst, "hst"), (ctail, "ctail"), (carry, "carry")):
                S.op("dve", lambda e, tns=tns: e.memset(tns[:], 0.0), writes=[key])
            cc_cnt[0] = 0
            S.op("dve", lambda e: e.memset(Wa[:], 0.0), writes=["Wa"])
            S.op("dve", lambda e: e.memset(Wi[:], 0.0), writes=["Wi"])

            def norm_to_hT(gt, gcol, Dn=D):
                for kc in range(16):
                    s_ = sq[kc % 2]
                    actf(s_[:], xT[:, kc, :], AF.Square, [("xT", kc)], [("sq", kc % 2)])
                    mm(ps[7][:, :], ones_bf[:], s_[:], kc == 0, kc == 15, [("sq", kc % 2), "ones_bf"], ["ps7"])
                actf(rstd[:], ps[7][:, :], AF.Ln, ["ps7"], ["rstd"], scale=1.0 / Dn, bias=EPS)
                actf(rstd[:], rstd[:], AF.Exp, ["rstd"], ["rstd"], scale=-0.5)

            def dense_fm(cols0, ncols, wsrc_v, evac):
                def fill(slot, c0=cols0, src=wsrc_v):
                    return [(slot[:, 0:16, :], src[FM_IDX[c0]])]
                w, wk = wq.req("A", fill)
                b = dbank()
                bk = "ps%d" % b
                for kc in range(16):
                    mm(ps[b][0:ncols, :], w[:, kc, 0:ncols], hT[:, kc, :], kc == 0, kc == 15, [wk, ("hT", kc)], [bk])
                evac(ps[b], bk)

            for l in range(NL):
                S.dma("sp", [(P[:], p128[l]), (fb[:], fbias[l].partition_broadcast(128)), (gb[0:1, :], gbias[l:l + 1, :])],
                      "par", writes=["P", "fb", "gb"])
                lwa_v = lwa[l].rearrange("(c e) d f -> e d c f", e=2)
                lwi_v = lwi[l].rearrange("(c e) d f -> e d c f", e=2)
                S.dma("pool", [(w2[:], gw2[l]), (Wa[0:64, :, 0:64], lwa_v[0]), (Wa[64:128, :, 64:128], lwa_v[1]),
                               (Wi[0:64, :, 0:64], lwi_v[0]), (Wi[64:128, :, 64:128], lwi_v[1])], "par2", writes=["w2", "Wa", "Wi"])
                actf(cl[:], P[:, C_LAM:C_LAM + 8], AF.Exp, ["P"], ["cl"], scale=-1.0)
                actf(cl[:], cl[:], AF.Ln, ["cl"], ["cl"], bias=1.0)
                S.op("dve", lambda e: e.tensor_scalar(out=cl[:], in0=cl[:], scalar1=-8.0, scalar2=None, op0=ALU.mult), reads=["cl"], writes=["cl"])
                for tns, key in ((S_f[:].rearrange("p a b -> p (a b)"), "S_f"), (hst[:], "hst"), (ctail[:].rearrange("p a b -> p (a b)"), "ctail"), (carry[:], "carry")):
                    S.op("dve", lambda e, tns=tns: e.tensor_scalar(out=tns, in0=tns, scalar1=pf[:, 0:1], scalar2=None, op0=ALU.mult), reads=[key, "pf"], writes=[key])
                actf(S_bf[:].rearrange("p a b -> p (a b)"), S_f[:].rearrange("p a b -> p (a b)"), AF.Copy, ["S_f"], ["S_bf"])
                w_in_v = w_in_fm[l]
                w_tm_v = w_in_tm[l]
                w_gate_v = w_gate_r[l]
                w_up_v = w_up_r[l]

                for t in range(NT):
                    t0 = t * T
                    nkb = NB + 4 * (t + 1)
                    kb0 = NB + 4 * t
                    src_v = xin_v if l == 0 else xs_v
                    for kc in range(16):
                        S.dma("sp", [(xT[:, kc, :], src_v[:, kc, t0:t0 + T])], ("xld", kc), reads=([("xs", t, kc)] if l > 0 else []), writes=[("xT", kc)])
                    norm_to_hT(P, C_GMIX)
                    for kc in range(16):
                        S.op("dve", lambda e: e.scalar_tensor_tensor(out=hT[:, kc, :], in0=xT[:, kc, :], scalar=P[:, C_GMIX + kc:C_GMIX + kc + 1],
                                                                      in1=rstd[:], op0=ALU.mult, op1=ALU.mult),
                             reads=[("xT", kc), "P", "rstd"], writes=[("hT", kc)])
                    S.barrier()
                    slabs = [("fvA", 256, [(0, O_FV, 256)]), ("fvB", 256, [(0, O_FV + 256, 256)]),
                             ("ffgk", 264, [(0, O_FF, 8), (8, O_GK, 256)]),
                             ("gvA", 256, [(0, O_GV, 256)]), ("gvB", 256, [(0, O_GV + 256, 256)])]
                    for si_, (nm, ncols, parts) in enumerate(slabs):
                        def fill(slot, si_=si_, wv=w_tm_v):
                            return [(slot[:, :, :], wv[si_])]
                        w, wk = wq.req("B", fill)
                        for j in range(4):
                            b = dbank()
                            bk = "ps%d" % b
                            blk = kb0 + j
                            for kc in range(16):
                                mm(ps[b][:, 0:ncols], hT[:, kc, j * 128:(j + 1) * 128], w[:, kc, 0:ncols], kc == 0, kc == 15, [wk, ("hT", kc)], [bk])
                            if nm in ("fvA", "fvB"):
                                h0 = 0 if nm == "fvA" else 4
                                actf(vc[:, blk, h0:h0 + 4, 0:64], ps[b][:, 0:256].rearrange("p (h d) -> p h d", d=64), AF.Copy, [bk], [("vc", blk)])
                            elif nm == "ffgk":
                                S.op("dve", lambda e: e.tensor_copy(out=ffraw[:, j, :], in_=ps[b][:, 0:8]), reads=[bk], writes=[("ffraw", j)])
                                actf(gk_tm[:, j, :], ps[b][:, 8:264], AF.Copy, [bk], [("gk_tm", j)])
                            else:
                                c0 = 0 if nm == "gvA" else 256
                                actf(gv_tm[:, j, c0:c0 + 256], ps[b][:, 0:256], AF.Copy, [bk], [("gv_tm", j)])
                    FR = [("ffraw", j) for j in range(4)]
                    for j in range(4):
                        S.op("dve", lambda e: e.tensor_tensor(out=spf[:, j, :], in0=ffraw[:, j, :], in1=fb[:], op=ALU.add), reads=[("ffraw", j), "fb"], writes=["spf"])
                    actf(spf[:], spf[:], AF.Exp, ["spf"], ["spf"], scale=-1.0)
                    actf(spf[:], spf[:], AF.Ln, ["spf"], ["spf"], bias=1.0)
                    for j in range(4):
                        for j2 in range(j):
                            mm(ps[7][:, j * 8:(j + 1) * 8], ones_f[:], spf[:, j2, :], j2 == 0, False, ["spf", "ones_f"], ["ps7"])
                        mm(ps[7][:, j * 8:(j + 1) * 8], tri_f[:], spf[:, j, :], j == 0, True, ["spf", "tri_f"], ["ps7"])
                    for j in range(4):
                        mm(ps[7][:, 32:40], ones_f[:], spf[:, j, :], j == 0, j == 3, ["spf", "ones_f"], ["ps7"])
                    for j in range(2):
                        mm(ps[7][:, 40:48], ones_f[:], spf[:, j, :], j == 0, j == 1, ["spf", "ones_f"], ["ps7"])
                    for j in range(4):
                        S.op("dve", lambda e: e.tensor_tensor(out=call[:, kb0 + j, :], in0=carry[:], in1=ps[7][:, j * 8:(j + 1) * 8], op=ALU.subtract),
                             reads=["carry", "ps7"], writes=["call"])
                    S.op("dve", lambda e: e.tensor_tensor(out=cref[:], in0=carry[:], in1=ps[7][:, 40:48], op=ALU.subtract), reads=["carry", "ps7"], writes=["cref"])
                    S.op("dve", lambda e: e.tensor_tensor(out=carry[:], in0=carry[:], in1=ps[7][:, 32:40], op=ALU.subtract), reads=["carry", "ps7"], writes=["carry"])
                    for h in range(8):
                        S.op("dve", lambda e: e.tensor_scalar(out=fbiasT[:, 0:nkb, h], in0=call[:, 0:nkb, h], scalar1=cref[:, h:h + 1], scalar2=-1.0,
                                                               op0=ALU.subtract, op1=ALU.mult), reads=["call", "cref"], writes=["fbiasT"])
                    S.op("dve", lambda e: e.tensor_scalar(out=fbiasT[:, 0:NB, :], in0=fbiasT[:, 0:NB, :], scalar1=pf[:, 1:2], scalar2=None, op0=ALU.add),
                         reads=["fbiasT", "pf"], writes=["fbiasT"])
                    def fox_gen():
                        for hp in range(4):
                            q_ = qT[hp % 2]
                            qk_ = ("qT", hp % 2)
                            dense_fm(O_FQ + hp * 128, 128, w_in_v, lambda p_, bk: actf(q_[:], p_[:, :], AF.Copy, [bk], [qk_]))
                            yield
                            dense_fm(O_FK + hp * 128, 128, w_in_v, lambda p_, bk: actf(kT[:, hp, SEQ + t0:SEQ + t0 + T], p_[:, :], AF.Copy, [bk], [("kT", hp)]))
                            yield
                            for e_ in range(2):
                                h = 2 * hp + e_
                                lo = e_ * 64

                                def qk(kb):
                                    sbk = 4 + kb % 2
                                    key = "ps%d" % sbk
                                    diag = kb >= kb0
                                    mm(ps[sbk][:, :], kT[lo:lo + 64, hp, kb * 128:(kb + 1) * 128], q_[lo:lo + 64, :], True, not diag, [("kT", hp), qk_], [key])
                                    if diag:
                                        r = kb - kb0
                                        mm(ps[sbk][:, :], ident_bf[:], fmask[:, 384 - r * 128:384 - r * 128 + 512], False, True, ["ident_bf", "fmask"], [key])
                                    actf(PT[kb % 3][:], ps[sbk][:, :], AF.Exp, [key, "fbiasT"], [("PT", kb % 3)], scale=0.125, bias=fbiasT[:, kb, h:h + 1])

                                def pv(kb):
                                    mm(ps[6][0:65, :], vc[:, kb, h, :], PT[kb % 3][:], kb == 0, kb == nkb - 1, [("vc", kb), "vc1", ("PT", kb % 3)], ["ps6"])

                                qk(0)
                                for kb in range(nkb):
                                    if kb + 1 < nkb:
                                        qk(kb + 1)
                                    pv(kb)
                                    yield
                                S.op("dve", lambda e: e.reciprocal(out=rrow[64:65, :], in_=ps[6][64:65, :]), reads=["ps6"], writes=["rrow"])
                                mm(ps[7][0:64, :], ones_f[64:65, 0:64], rrow[64:65, :], True, True, ["ones_f", "rrow"], ["ps7"])
                                actf(osb[0:64, :], ps[6][0:64, :], AF.Copy, ["ps6"], ["osb"])
                                S.op("dve", lambda e: e.tensor_tensor(out=mix_fox[0:64, h, :], in0=osb[0:64, :], in1=ps[7][0:64, :], op=ALU.mult),
                                     reads=["osb", "ps7"], writes=[("mix_fox", h)])
                                yield
                        for h in range(8):
                            s_ = sq[h % 2]
                            actf(s_[0:64, :], mix_fox[0:64, h, :], AF.Square, [("mix_fox", h)], [("sq", h % 2)])
                            mm(ps[7][0:64, :], ones_bf[0:64, 0:64], s_[0:64, :], h == 0, h == 7, [("sq", h % 2), "ones_bf"], ["ps7"])
                        yield
                        actf(rstd[0:64, :], ps[7][0:64, :], AF.Ln, ["ps7"], ["rstd"], scale=1.0 / 512, bias=EPS)
                        actf(rstd[0:64, :], rstd[0:64, :], AF.Exp, ["rstd"], ["rstd"], scale=-0.5)
                        for h in range(8):
                            S.op("dve", lambda e: e.scalar_tensor_tensor(out=mix_fox[0:64, h, :], in0=mix_fox[0:64, h, :], scalar=P[0:64, C_GFOX + h:C_GFOX + h + 1],
                                                                          in1=rstd[0:64, :], op0=ALU.mult, op1=ALU.mult),
                                 reads=[("mix_fox", h), "P", "rstd"], writes=[("mix_fox", h)])
                            yield

                    def lru_gen():
                        for c in range(8):
                            lb = lxbuf[c % 2]
                            lk = ("lxbuf", c % 2)
                            S.op("dve", lambda e: e.tensor_copy(out=lb[:, 0:3], in_=ctail[:, c, 0:3]), reads=["ctail"], writes=[lk])
                            dense_fm(O_LX + c * 128, 128, w_in_v, lambda p_, bk: actf(lb[:, 3:515], p_[:, :], AF.Copy, [bk], [lk]))
                            S.op("dve", lambda e: e.tensor_copy(out=ctail[:, c, 0:3], in_=lb[:, 512:515]), reads=[lk], writes=["ctail"])
                            yield
                            dense_fm(O_LG + c * 128, 128, w_in_v, lambda p_, bk: actf(xg[:], p_[:, :], AF.Copy, [bk], ["xg"]))
                            yield
                            actf(tA[:], xg[:], AF.Square, ["xg"], ["tA"])
                            S.op("dve", lambda e: e.tensor_scalar(out=tA[:], in0=tA[:], scalar1=0.044715, scalar2=1.0, op0=ALU.mult, op1=ALU.add), reads=["tA"], writes=["tA"])
                            S.op("dve", lambda e: e.tensor_tensor(out=tA[:], in0=tA[:], in1=xg[:], op=ALU.mult), reads=["tA", "xg"], writes=["tA"])
                            yield
                            actf(tA[:], tA[:], AF.Sigmoid, ["tA"], ["tA"], scale=1.5957691216057308)
                            S.op("dve", lambda e: e.tensor_tensor(out=glu[:], in0=tA[:], in1=xg[:], op=ALU.mult), reads=["tA", "xg"], writes=["glu"])
                            cw = C_CW + c * 4
                            S.op("dve", lambda e: e.tensor_scalar(out=tB[:], in0=lb[:, 0:512], scalar1=P[:, cw:cw + 1], scalar2=P[:, C_CB + c:C_CB + c + 1],
                                                                   op0=ALU.mult, op1=ALU.add), reads=[lk, "P"], writes=["tB"])
                            yield
                            for jj in range(1, 4):
                                S.op("dve", lambda e: e.scalar_tensor_tensor(out=tB[:], in0=lb[:, jj:jj + 512], scalar=P[:, cw + jj:cw + jj + 1], in1=tB[:],
                                                                              op0=ALU.mult, op1=ALU.add), reads=[lk, "P", "tB"], writes=["tB"])
                            actf(lin_bf[:], tB[:], AF.Copy, ["tB"], ["lin_bf"])
                            yield
                            yield
                            yield
                            yield
                            mm(ps[2][:, :], Wa[:, c, :], lin_bf[:], True, True, ["Wa", "lin_bf"], ["ps2"])
                            mm(ps[3][:, :], Wi[:, c, :], lin_bf[:], True, True, ["Wi", "lin_bf"], ["ps3"])
                            actf(tC[:], ps[2][:, :], AF.Sigmoid, ["ps2", "P"], ["tC"], bias=P[:, C_BA + c:C_BA + c + 1])
                            actf(tD[:], ps[3][:, :], AF.Sigmoid, ["ps3", "P"], ["tD"], bias=P[:, C_BI + c:C_BI + c + 1])
                            yield
                            actf(tC[:], tC[:], AF.Exp, ["tC", "cl"], ["tC"], scale=cl[:, c:c + 1])
                            actf(tE[:], tC[:], AF.Square, ["tC"], ["tE"])
                            actf(tE[:], tE[:], AF.Sqrt, ["tE"], ["tE"], scale=-1.0, bias=1.0)
                            yield
                            S.op("dve", lambda e: e.tensor_tensor(out=tD[:], in0=tD[:], in1=tE[:], op=ALU.mult), reads=["tD", "tE"], writes=["tD"])
                            S.op("dve", lambda e: e.tensor_tensor(out=tD[:], in0=tD[:], in1=tB[:], op=ALU.mult), reads=["tD", "tB"], writes=["tD"])
                            S.op("dve", lambda e: e.tensor_tensor_scan(out=tE[:], data0=tC[:], data1=tD[:], initial=hst[:, c:c + 1], op0=ALU.mult, op1=ALU.add),
                                 reads=["tC", "tD", "hst"], writes=["tE"])
                            yield
                            S.op("dve", lambda e: e.tensor_copy(out=hst[:, c:c + 1], in_=tE[:, 511:512]), reads=["tE"], writes=["hst"])
                            S.op("dve", lambda e: e.tensor_tensor(out=tE[:], in0=tE[:], in1=glu[:], op=ALU.mult), reads=["tE", "glu"], writes=["tE"])
                            actf(mix_lru[:, c, :], tE[:], AF.Copy, ["tE"], [("mix_lru", c)])
                            actf(lsq[:], tE[:], AF.Square, ["tE"], ["lsq"])
                            yield
                            yield
                            yield
                            mm(ps[2][:, :], ones_bf[:], lsq[:], True, True, ["lsq", "ones_bf"], ["ps2"])
                            if c == 0:
                                S.op("dve", lambda e: e.tensor_copy(out=lssq[:], in_=ps[2][:, :]), reads=["ps2"], writes=["lssq"])
                            else:
                                S.op("dve", lambda e: e.tensor_tensor(out=lssq[:], in0=ps[2][:, :], in1=lssq[:], op=ALU.add), reads=["ps2", "lssq"], writes=["lssq"])
                            yield
                        actf(lssq[:], lssq[:], AF.Ln, ["lssq"], ["lssq"], scale=1.0 / 1024, bias=EPS)
                        actf(lssq[:], lssq[:], AF.Exp, ["lssq"], ["lssq"], scale=-0.5)
                        for c in range(8):
                            S.op("dve", lambda e: e.scalar_tensor_tensor(out=mix_lru[:, c, :], in0=mix_lru[:, c, :], scalar=P[:, C_GLRU + c:C_GLRU + c + 1], in1=lssq[:],
                                                                          op0=ALU.mult, op1=ALU.mult), reads=[("mix_lru", c), "P", "lssq"], writes=[("mix_lru", c)])
                            yield

                    DB[0] = [0, 1]
                    gens = [fox_gen(), lru_gen()]
                    ratio = [1, 1]
                    alive = [True, True]
                    while any(alive):
                        for gi in range(2):
                            for _ in range(ratio[gi]):
                                if alive[gi]:
                                    try:
                                        next(gens[gi])
                                    except StopIteration:
                                        alive[gi] = False
                    S.barrier()
                    DB[0] = [0, 1, 2, 3]
                    S.barrier()
                    dense_fm(O_GR, 16, w_in_v, lambda p_, bk: actf(grT[0:16, :], p_[0:16, :], AF.Copy, [bk], ["grT"]))
                    for j in range(4):
                        pb = 4 + j // 2
                        c0 = (j % 2) * 256
                        mm(ps[pb][:, c0:c0 + 256], grT[0:16, j * 128:(j + 1) * 128], w2[0:16, :], True, False, ["grT", "w2"], ["ps%d" % pb])
                        mm(ps[pb][:, c0:c0 + 256], ones_f[0:1, 0:128], gb[0:1, :], False, True, ["ones_f", "gb"], ["ps%d" % pb])
                    spg2 = spg[:].rearrange("p j c -> p (j c)")
                    for hf in range(2):
                        actf(spg2[:, hf * 512:(hf + 1) * 512], ps[4 + hf][:, :], AF.Exp, ["ps%d" % (4 + hf)], ["spg"], scale=-1.0)
                    actf(spg2, spg2, AF.Ln, ["spg"], ["spg"], bias=1.0)
                    for pr in range(2):
                        for j in range(4):
                            mm(ps[4 + pr][:, j * 128:(j + 1) * 128], spg[:, j, pr * 128:(pr + 1) * 128], tri_f[:], True, True, ["spg", "tri_f"], ["ps%d" % (4 + pr)])
                    for j in range(4):
                        pb = 6 + j // 2
                        c0 = (j % 2) * 256
                        mm(ps[pb][:, c0:c0 + 256], triu_f[:], spg[:, j, :], True, True, ["spg", "triu_f"], ["ps%d" % pb])
                    for pr in range(2):
                        actf(tmpE[:, pr * 512:(pr + 1) * 512], ps[4 + pr][:, :], AF.Exp, ["ps%d" % (4 + pr)], ["tmpE"], scale=-1.0 / 16)
                    for pr in range(2):
                        for j in range(4):
                            c_ = pr * 512 + j * 128 + 127
                            S.op("dve", lambda e: e.tensor_copy(out=eblast[:, pr, j:j + 1], in_=tmpE[:, c_:c_ + 1]), reads=["tmpE"], writes=["eblast"])
                    for pr in range(2):
                        dense_fm(O_GQ + pr * 128, 128, w_in_v, lambda p_, bk: S.op("dve", lambda e: e.scalar_tensor_tensor(
                            out=Qt[:, pr, :], in0=p_[:, :], scalar=0.125, in1=tmpE[:, pr * 512:(pr + 1) * 512], op0=ALU.mult, op1=ALU.mult),
                            reads=[bk, "tmpE"], writes=["Qt"]))
                    for pr in range(2):
                        actf(tmpE[:, pr * 512:(pr + 1) * 512], ps[4 + pr][:, :], AF.Exp, ["ps%d" % (4 + pr)], ["tmpE"], scale=1.0 / 16)
                    for pr in range(2):
                        dense_fm(O_GK + pr * 128, 128, w_in_v, lambda p_, bk: S.op("dve", lambda e: e.tensor_tensor(
                            out=Kt[:, pr, :], in0=p_[:, :], in1=tmpE[:, pr * 512:(pr + 1) * 512], op=ALU.mult), reads=[bk, "tmpE"], writes=["Kt"]))
                    for hf in range(2):
                        actf(tmpE[:, hf * 512:(hf + 1) * 512], ps[6 + hf][:, :], AF.Exp, ["ps%d" % (6 + hf)], ["tmpE"], scale=-1.0 / 16)
                    S.op("dve", lambda e: e.tensor_tensor(out=Khat[:].rearrange("p j c -> p (j c)"), in0=gk_tm[:].rearrange("p j c -> p (j c)"), in1=tmpE[:, :], op=ALU.mult),
                         reads=["tmpE"] + [("gk_tm", j) for j in range(4)], writes=["Khat"])
                    for h in range(4):
                        dense_fm(O_GG + h * 128, 128, w_in_v, lambda p_, bk: actf(gsilu[:, h, :], p_[:, :], AF.Silu, [bk], ["gsilu"]))
                    GV = [("gv_tm", j) for j in range(4)]
                    k_ = 0
                    for j in range(4):
                        js = slice(j * 128, (j + 1) * 128)
                        for pr in range(2):
                            for e_ in range(2):
                                h = 2 * pr + e_
                                lo = e_ * 64
                                a_ = ATb[k_ % 2]
                                ak = ("AT", k_ % 2)
                                c4 = (k_ % 4) * 128
                                k_ += 1
                                mm(ps[4][:, c4:c4 + 128], Kt[lo:lo + 64, pr, js], Qt[lo:lo + 64, pr, js], True, True, ["Kt", "Qt"], [("ps4", c4)])
                                S.op("dve", lambda e: e.tensor_tensor(out=a_[:], in0=ps[4][:, c4:c4 + 128], in1=tri_f[:], op=ALU.mult), reads=[("ps4", c4), "tri_f"], writes=[ak])
                                mm(ps[5][:, c4:c4 + 128], gv_tm[:, j, h * 128:(h + 1) * 128], a_[:], True, False, [("gv_tm", j), ak], [("ps5", c4)])
                                mm(ps[5][:, c4:c4 + 128], S_bf[lo:lo + 64, pr, :], Qt[lo:lo + 64, pr, js], False, True, ["S_bf", "Qt"], [("ps5", c4)])
                                actf(gla_o[:, h, js], ps[5][:, c4:c4 + 128], AF.Copy, [("ps5", c4)], [("gla_o", h)])
                                mm(ps[6][lo:lo + 64, pr * 128:(pr + 1) * 128], Khat[:, j, h * 64:(h + 1) * 64], gv_tm[:, j, h * 128:(h + 1) * 128], True, True,
                                   ["Khat", ("gv_tm", j)], [("ps6", pr)])
                            S.op("dve", lambda e: e.scalar_tensor_tensor(out=S_f[:, pr, :], in0=S_f[:, pr, :], scalar=eblast[:, pr, j:j + 1],
                                                                          in1=ps[6][:, pr * 128:(pr + 1) * 128], op0=ALU.mult, op1=ALU.add),
                                 reads=["S_f", "eblast", ("ps6", pr)], writes=["S_f"])
                            actf(S_bf[:, pr, :], S_f[:, pr, :], AF.Copy, ["S_f"], ["S_bf"])
                    for h in range(4):
                        s_ = sq[h % 2]
                        actf(s_[:], gla_o[:, h, :], AF.Square, [("gla_o", h)], [("sq", h % 2)])
                        mm(ps[7][:, :], ones_bf[:], s_[:], True, True, [("sq", h % 2), "ones_bf"], ["ps7"])
                        actf(rstd[:], ps[7][:, :], AF.Ln, ["ps7"], ["rstd"], scale=1.0 / 128, bias=EPS)
                        actf(rstd[:], rstd[:], AF.Exp, ["rstd"], ["rstd"], scale=-0.5)
                        S.op("dve", lambda e: e.scalar_tensor_tensor(out=gla_o[:, h, :], in0=gla_o[:, h, :], scalar=P[:, C_GGLA:C_GGLA + 1], in1=rstd[:],
                                                                      op0=ALU.mult, op1=ALU.mult), reads=[("gla_o", h), "P", "rstd"], writes=[("gla_o", h)])
                        S.op("dve", lambda e: e.tensor_tensor(out=mix_gla[:, h, :], in0=gla_o[:, h, :], in1=gsilu[:, h, :], op=ALU.mult),
                             reads=[("gla_o", h), "gsilu"], writes=[("mix_gla", h)])
                    S.barrier()
                    S.barrier()
                    if t == NT - 1 and l < NL - 1:
                        sb_v = snd_b.ap()
                        sv_v = snd_v.ap()
                        sf_v = snd_f.ap()
                        o1 = 4 * SEQ
                        S.dma("sp", [(sb_v[:, 0:o1].rearrange("p (h s) -> p h s", h=4), kT[:, :, SEQ:2 * SEQ]),
                                     (sv_v[:, :].rearrange("p (b h d) -> p b h d", b=NB, h=8), vc[:, NB:NBLK, :, :]),
                                     (sf_v[:, 0:NB * 8].rearrange("p (b h) -> p b h", h=8), call[:, NB:NBLK, :]),
                                     (sf_v[:, NB * 8:NB * 8 + 8], carry[:]),
                                     (sf_v[:, NB * 8 + 8:NB * 8 + 264].rearrange("p (a b) -> p a b", a=2), S_f[:]),
                                     (sf_v[:, NB * 8 + 264:NB * 8 + 272], hst[:]),
                                     (sf_v[:, NB * 8 + 272:NB * 8 + 304].rearrange("p (a b) -> p a b", a=8), ctail[:])],
                              "snd", reads=[("kT", h_) for h_ in range(4)] + [("vc", b_) for b_ in range(NBLK)] + ["call", "carry", "S_f", "hst", "ctail"],
                              writes=["snd"])
                        if not S.dry:
                            rg = [[2 * i_, 2 * i_ + 1] for i_ in range(ncores // 2)]
                            S._deps("pool", ["snd"], ["rcv"])
                            nc.gpsimd.collective_compute("AllGather", ALU.bypass, replica_groups=rg, ins=[snd_b.ap().opt()], outs=[rcv_b.ap().opt()]).then_inc(cc_sem)
                            nc.gpsimd.collective_compute("AllGather", ALU.bypass, replica_groups=rg, ins=[snd_v.ap().opt()], outs=[rcv_v.ap().opt()]).then_inc(cc_sem)
                            nc.gpsimd.collective_compute("AllGather", ALU.bypass, replica_groups=rg, ins=[snd_f.ap().opt()], outs=[rcv_f.ap().opt()]).then_inc(cc_sem)
                            cc_cnt[0] += 3
                            S._record((cc_sem, cc_cnt[0], "dma"), ["snd"], ["rcv"])
                        rb_v = rcv_b.ap()
                        rv_v = rcv_v.ap()
                        rf_v = rcv_f.ap()
                        S.dma("sp", [(kT[:, :, 0:SEQ], rb_v[0:128, 0:o1].rearrange("p (h s) -> p h s", h=4)),
                                     (vc[:, 0:NB, :, :], rv_v[0:128, :].rearrange("p (b h d) -> p b h d", b=NB, h=8)),
                                     (call[:, 0:NB, :], rf_v[0:128, 0:NB * 8].rearrange("p (b h) -> p b h", h=8)),
                                     (carry[:], rf_v[0:128, NB * 8:NB * 8 + 8]),
                                     (S_f[:], rf_v[0:128, NB * 8 + 8:NB * 8 + 264].rearrange("p (a b) -> p a b", a=2)),
                                     (hst[:], rf_v[0:128, NB * 8 + 264:NB * 8 + 272]),
                                     (ctail[:], rf_v[0:128, NB * 8 + 272:NB * 8 + 304].rearrange("p (a b) -> p a b", a=8))],
                              "rcv", reads=["rcv"],
                              writes=[("kT", h_) for h_ in range(4)] + [("vc", b_) for b_ in range(NB)] + ["call", "carry", "S_f", "hst", "ctail"])
                    if dbg and l == 0 and t == NT - 1:
                        def dump(nm, src, n):
                            for i in range(n):
                                actf(tA[:], src(i), AF.Copy, ["dbgsrc"], ["tA"])
                                S.dma("sp", [(dbg_out[nm][:, i * 512:(i + 1) * 512], tA[:])], "dbg", reads=["tA"])
                        dump("d_fox", lambda i: mix_fox[:, i, :], 8)
                        dump("d_gla", lambda i: mix_gla[:, i, :], 4)
                        dump("d_lru", lambda i: mix_lru[:, i, :], 8)
                        S.barrier()
                    MK = [("mix_fox", h) for h in range(8)] + [("mix_gla", h) for h in range(4)] + [("mix_lru", c) for c in range(8)]
                    for oc in range(16):
                        cs = slice(oc * 128, (oc + 1) * 128)

                        def fill(slot, oc=oc, l=l):
                            return [(slot[:, :, :], w_out_r[l, oc])]
                        w, wk = wq.req("A", fill)
                        b = dbank()
                        bk = "ps%d" % b
                        for h in range(8):
                            mm(ps[b][:, :], w[0:64, h, :], mix_fox[0:64, h, :], h == 0, False, [wk, ("mix_fox", h)], [bk])
                        for h in range(4):
                            mm(ps[b][:, :], w[:, 8 + h, :], mix_gla[:, h, :], False, False, [wk, ("mix_gla", h)], [bk])
                        for c in range(8):
                            mm(ps[b][:, :], w[:, 12 + c, :], mix_lru[:, c, :], False, c == 7, [wk, ("mix_lru", c)], [bk])
                        S.op("dve", lambda e: e.tensor_tensor(out=xT[:, oc, :], in0=ps[b][:, :], in1=xT[:, oc, :], op=ALU.add), reads=[bk, ("xT", oc)], writes=[("xT", oc)])
                    if dbg and l == 0 and t == NT - 1:
                        S.dma("sp", [(dbg_out["d_x1"].rearrange("p (k s) -> p k s", k=16), xT[:])], "dbg2", reads=XK)
                    norm_to_hT(P, C_GFFN)
                    for kc in range(16):
                        S.op("dve", lambda e: e.scalar_tensor_tensor(out=hT[:, kc, :], in0=xT[:, kc, :], scalar=P[:, C_GFFN + kc:C_GFFN + kc + 1],
                                                                      in1=rstd[:], op0=ALU.mult, op1=ALU.mult),
                             reads=[("xT", kc), "P", "rstd"], writes=[("hT", kc)])
                    for hh in range(2):
                        for hl in range(22):
                            hc = hh * 22 + hl
                            res = {}
                            for nm, wv in (("g", w_gate_v), ("u", w_up_v)):
                                def fill(slot, wv=wv, hc=hc):
                                    return [(slot[:, 0:16, :], wv[hc])]
                                w, wk = wq.req("A", fill)
                                b = dbank()
                                for kc in range(16):
                                    mm(ps[b][:, :], w[:, kc, :], hT[:, kc, :], kc == 0, kc == 15, [wk, ("hT", kc)], ["ps%d" % b])
                                res[nm] = b
                            s_ = sg[hl % 2]
                            actf(s_[:], ps[res["g"]][:, :], AF.Silu, ["ps%d" % res["g"]], [("sg", hl % 2)])
                            S.op("dve", lambda e: e.tensor_tensor(out=act[:, hl, :], in0=s_[:], in1=ps[res["u"]][:, :], op=ALU.mult),
                                 reads=[("sg", hl % 2), "ps%d" % res["u"]], writes=[("act", hl)])
                        for oc in range(16):
                            def fill(slot, hh=hh, oc=oc, l=l):
                                return [(slot[:, 0:22, :], w_down_r[l, hh, oc])]
                            w, wk = wq.req("D", fill)
                            b = dbank()
                            bk = "ps%d" % b
                            for hl in range(22):
                                mm(ps[b][:, :], w[:, hl, :], act[:, hl, :], hl == 0, hl == 21, [wk, ("act", hl)], [bk])
                            S.op("dve", lambda e: e.tensor_tensor(out=xT[:, oc, :], in0=ps[b][:, :], in1=xT[:, oc, :], op=ALU.add), reads=[bk, ("xT", oc)], writes=[("xT", oc)])
                    if l < NL - 1:
                        for kc in range(16):
                            S.dma("sp", [(xs_v[:, kc, t0:t0 + T], xT[:, kc, :])], ("xst", kc), reads=[("xT", kc)], writes=[("xs", t, kc)])
                    else:
                        norm_to_hT(GF, 0)
                        for kc in range(16):
                            o_ = ost[kc % 2]
                            S.op("dve", lambda e: e.scalar_tensor_tensor(out=o_[:], in0=xT[:, kc, :], scalar=GF[:, kc:kc + 1], in1=rstd[:], op0=ALU.mult, op1=ALU.mult),
                                 reads=[("xT", kc), "GF", "rstd"], writes=[("ost", kc % 2)])
                            S.dma("sp", [(outT[kc * 128:(kc + 1) * 128, t0:t0 + T], o_[:])], ("ost", kc % 2), reads=[("ost", kc % 2)])
                    S.barrier(dma_keys=[("ost", 0), ("ost", 1), "dbg", "dbg2"])

        S.dry = True
        emit()
        S.dry = False
        emit()
        S.finish()
    return nc


def host_inputs(inputs, NL):
    f = lambda a: np.ascontiguousarray(np.asarray(a, dtype=np.float32))
    p = np.zeros((NL, 128, NP), np.float32)
    p[:, :, C_GMIX:C_GMIX + 16] = f(inputs["norm_mix"])[:NL].reshape(NL, 16, 128).transpose(0, 2, 1)
    p[:, :, C_GFFN:C_GFFN + 16] = f(inputs["norm_ffn"])[:NL].reshape(NL, 16, 128).transpose(0, 2, 1)
    p[:, 0:64, C_GFOX:C_GFOX + 8] = f(inputs["fox_out_norm"])[:NL].reshape(NL, 8, 64).transpose(0, 2, 1)
    p[:, :, C_GGLA] = f(inputs["gla_head_norm"])[:NL]
    p[:, :, C_CW:C_CW + 32] = f(inputs["conv_w"])[:NL].reshape(NL, 4, 8, 128).transpose(0, 3, 2, 1).reshape(NL, 128, 32)
    for nm, c in (("conv_b", C_CB), ("lru_b_a", C_BA), ("lru_b_i", C_BI), ("lru_lambda", C_LAM), ("lru_out_norm", C_GLRU)):
        p[:, :, c:c + 8] = f(inputs[nm])[:NL].reshape(NL, 8, 128).transpose(0, 2, 1)
    k = np.arange(128)
    tri = (k[:, None] <= k[None, :]).astype(np.float32)
    triu = (k[:, None] > k[None, :]).astype(np.float32)
    jj = np.arange(896)
    mask = np.where((jj[None, :] - 384) >= k[:, None], 0.0, NEG).astype(np.float32)
    f32 = lambda a: np.asarray(a, dtype=np.float32)
    w_in = f32(inputs["w_in"])[:NL].reshape(NL, 16, 128, INW)
    fm = np.zeros((NL, NFM, 128, 16, 128), np.float32)
    for i, c0 in enumerate(FM_COLS):
        n = 16 if c0 == O_GR else 128
        fm[:, i, :, :, :n] = w_in[:, :, :, c0:c0 + n].transpose(0, 2, 1, 3)
    tm = np.zeros((NL, 5, 128, 16, 264), np.float32)
    for i, parts in enumerate(TM_SLABS):
        for (d0, s0, n) in parts:
            tm[:, i, :, :, d0:d0 + n] = w_in[:, :, :, s0:s0 + n].transpose(0, 2, 1, 3)
    w_out = f32(inputs["w_out"])[:NL]
    wo = np.zeros((NL, 16, 128, 20, 128), np.float32)
    wo[:, :, 0:64, 0:8, :] = w_out[:, 0:512].reshape(NL, 8, 64, 16, 128).transpose(0, 3, 2, 1, 4)
    wo[:, :, :, 8:12, :] = w_out[:, 512:1024].reshape(NL, 4, 128, 16, 128).transpose(0, 3, 2, 1, 4)
    wo[:, :, :, 12:20, :] = w_out[:, 1024:2048].reshape(NL, 8, 128, 16, 128).transpose(0, 3, 2, 1, 4)
    wg = f32(inputs["w_gate"])[:NL].reshape(NL, 16, 128, 44, 128).transpose(0, 3, 2, 1, 4)
    wu = f32(inputs["w_up"])[:NL].reshape(NL, 16, 128, 44, 128).transpose(0, 3, 2, 1, 4)
    wd = f32(inputs["w_down"])[:NL].reshape(NL, 2, 22, 128, 16, 128).transpose(0, 1, 4, 3, 2, 5)
    layered = {
        "w_in_fm": fm, "w_in_tm": tm, "w_out_r": wo, "w_gate_r": wg, "w_up_r": wu, "w_down_r": wd, "p128": p,
        "fbias": f(inputs["fox_f_bias"])[:NL], "gw2": f(inputs["gla_gate_w2"])[:NL], "gbias": f(inputs["gla_gate_bias"])[:NL],
        "lwa": f(inputs["lru_w_a"])[:NL], "lwi": f(inputs["lru_w_i"])[:NL],
    }
    consts = {
        "gfin": np.ascontiguousarray(f(inputs["final_norm"]).reshape(16, 128).T),
        "c_tri": tri, "c_triu": triu, "c_ident": np.eye(128, dtype=np.float32), "c_mask": mask,
    }
    return layered, consts


def shifted(layered, par):
    out = {}
    for k_, a in layered.items():
        z = np.zeros((1,) + a.shape[1:], np.float32)
        if k_ == "fbias":
            z[:] = 30.0
        out[k_] = np.concatenate([a, z], 0) if par == 0 else np.concatenate([z, a], 0)
    return out


_NC_CACHE = {}


def kernel(**inputs):
    x = np.asarray(inputs["x"], dtype=np.float32)
    B, SEQ, _ = x.shape
    NL = int(np.asarray(inputs["w_in"]).shape[0])
    SEQC = SEQ // 2
    layered, consts = host_inputs(inputs, NL)
    key = (NL + 1, SEQC, 2 * B)
    if key not in _NC_CACHE:
        _NC_CACHE[key] = build_nc(NL + 1, SEQC, ncores=2 * B)
    nc = _NC_CACHE[key]
    per_par = [shifted(layered, 0), shifted(layered, 1)]
    pflags = [np.tile(np.array([[0.0, NEG]], np.float32), (128, 1)), np.tile(np.array([[1.0, 0.0]], np.float32), (128, 1))]
    in_maps = []
    for b in range(B):
        for par in range(2):
            m = dict(consts)
            m.update(per_par[par])
            m["pflag"] = pflags[par]
            m["xT"] = np.ascontiguousarray(x[b, par * SEQC:(par + 1) * SEQC].T)
            in_maps.append(m)
    res = run_bass_kernel_spmd(nc, in_maps, core_ids=list(range(2 * B)))
    out = np.empty((B, SEQ, D), np.float32)
    for b in range(B):
        for par in range(2):
            out[b, par * SEQC:(par + 1) * SEQC] = np.asarray(res.results[2 * b + par]["outT"], dtype=np.float32).T
    return out
```

```python
import contextlib
import numpy as np
import concourse.bass as bass
import concourse.mybir as mybir
from concourse.bass_utils import run_bass_kernel_spmd

F32 = mybir.dt.float32
BF16 = mybir.dt.bfloat16
AF = mybir.ActivationFunctionType
ALU = mybir.AluOpType

D = 2048
INW = 5144
FFN = 5632
NKC = 16
T = 512
EPS = 1e-6
O_FQ, O_FK, O_FV, O_FF, O_GQ, O_GK, O_GV, O_GG, O_GR, O_LG, O_LX = 0, 512, 1024, 1536, 1544, 1800, 2056, 2568, 3080, 3096, 4120
C_GMIX, C_GFFN, C_GFOX, C_GGLA, C_CW, C_CB, C_BA, C_BI, C_LAM, C_GLRU = 0, 16, 32, 40, 41, 73, 81, 89, 97, 105
NP = 113
NEG = -30000.0
FM_COLS = []
for _hp in range(4):
    FM_COLS += [O_FQ + _hp * 128, O_FK + _hp * 128]
FM_COLS += [O_GR] + [O_GQ, O_GQ + 128, O_GK, O_GK + 128] + [O_GG + _h * 128 for _h in range(4)]
for _c in range(8):
    FM_COLS += [O_LX + _c * 128, O_LG + _c * 128]
FM_IDX = {c: i for i, c in enumerate(FM_COLS)}
NFM = len(FM_COLS)
TM_SLABS = [[(0, O_FV, 256)], [(0, O_FV + 256, 256)], [(0, O_FF, 8), (8, O_GK, 256)], [(0, O_GV, 256)], [(0, O_GV + 256, 256)]]
SB_BASE = 16640
SB_END = 229376


class Sched:
    def __init__(self, nc, es):
        self.nc = nc
        self.es = es
        self.eng = {"pe": nc.tensor, "act": nc.scalar, "dve": nc.vector, "pool": nc.gpsimd, "sp": nc.sync}
        self.sem = {e: es.enter_context(nc.semaphore("sem_" + e)) for e in self.eng}
        self.cnt = {e: 0 for e in self.eng}
        self.waited = {e: {} for e in self.eng}
        self.lastw = {}
        self.readers = {}
        self.dmasem = {}
        self.dmacnt = {}
        self.dry = False

    def _wait(self, e, tok):
        if tok is None:
            return
        sem, val, src = tok
        if src == e and e == "pe":
            return
        w = self.waited[e]
        if w.get(id(sem), 0) >= val:
            return
        w[id(sem)] = val
        self.eng[e].wait_ge(sem, val)

    def _deps(self, e, reads, writes):
        for k in reads:
            self._wait(e, self.lastw.get(k))
        for k in writes:
            self._wait(e, self.lastw.get(k))
            for t in self.readers.get(k, {}).values():
                self._wait(e, t)

    def _record(self, tok, reads, writes):
        for k in reads:
            self.readers.setdefault(k, {})[id(tok[0])] = tok
        for k in writes:
            self.lastw[k] = tok
            self.readers[k] = {}

    def op(self, e, fn, reads=(), writes=()):
        if self.dry:
            return None
        self._deps(e, reads, writes)
        ins = fn(self.eng[e])
        self.cnt[e] += 1
        ins.then_inc(self.sem[e], 1)
        tok = (self.sem[e], self.cnt[e], e)
        self._record(tok, reads, writes)
        return tok

    def dma(self, q, pairs, semkey, reads=(), writes=()):
        if self.dry:
            return None
        self._deps(q, reads, writes)
        if semkey not in self.dmasem:
            self.dmasem[semkey] = self.es.enter_context(self.nc.semaphore("dsem_%d" % len(self.dmasem)))
            self.dmacnt[semkey] = 0
        sem = self.dmasem[semkey]
        for (o, i) in pairs:
            if q == "pool":
                self.eng[q].dma_start(out=o, in_=i, max_dma_last_dim=8192).then_inc(sem, 16)
            else:
                self.eng[q].dma_start(out=o, in_=i).then_inc(sem, 16)
            self.dmacnt[semkey] += 16
        tok = (sem, self.dmacnt[semkey], "dma")
        self._record(tok, reads, writes)
        return tok

    def barrier(self, engines=("pe", "act", "dve"), dma_keys=()):
        if self.dry:
            return
        toks = [(self.sem[e], self.cnt[e], e) for e in engines if self.cnt[e] > 0]
        for k in dma_keys:
            if k in self.dmasem:
                toks.append((self.dmasem[k], self.dmacnt[k], "dma"))
        for e in engines:
            for t in toks:
                if t[2] != e:
                    self._wait(e, t)

    def finish(self):
        for e in self.eng:
            if self.cnt[e] > 0 and e != "sp":
                self._wait("sp", (self.sem[e], self.cnt[e], e))
        for k, sem in self.dmasem.items():
            self._wait("sp", (sem, self.dmacnt[k], "dma"))


class WQ:
    def __init__(self, S, slots):
        self.S = S
        self.slots = slots
        self.plan = {c: [] for c in slots}
        self.reset()

    def reset(self):
        self.count = {c: 0 for c in self.slots}
        self.issued = {c: 0 for c in self.slots}

    def req(self, cls, fill):
        S = self.S
        slots = self.slots[cls]
        if S.dry:
            self.plan[cls].append(fill)
            return slots[0], (cls, 0)
        n = self.count[cls]
        depth = len(slots)
        plan = self.plan[cls]
        while self.issued[cls] < min(n + depth, len(plan)):
            j = self.issued[cls]
            s = j % depth
            S.dma("pool", plan[j](slots[s]), semkey=("w", cls, s), writes=[(cls, s)])
            self.issued[cls] += 1
        self.count[cls] += 1
        return slots[n % depth], (cls, n % depth)


def build_nc(NL, SEQ, dbg=False, ncores=8):
    NT = SEQ // T
    NB = SEQ // 128
    NBLK = 2 * NB
    nc = bass.Bass("TRN2", target_bir_lowering=False)

    def din(name, shape):
        return nc.dram_tensor(name, list(shape), F32, kind="ExternalInput").ap()

    xT_in = din("xT", [D, SEQ])
    w_in_fm = din("w_in_fm", [NL, NFM, 128, 16, 128])
    w_in_tm = din("w_in_tm", [NL, 5, 128, 16, 264])
    w_out_r = din("w_out_r", [NL, 16, 128, 20, 128])
    w_gate_r = din("w_gate_r", [NL, 44, 128, 16, 128])
    w_up_r = din("w_up_r", [NL, 44, 128, 16, 128])
    w_down_r = din("w_down_r", [NL, 2, 16, 128, 22, 128])
    p128 = din("p128", [NL, 128, NP])
    gfin = din("gfin", [128, 16])
    fbias = din("fbias", [NL, 8])
    gw2 = din("gw2", [NL, 16, 256])
    gbias = din("gbias", [NL, 256])
    lwa = din("lwa", [NL, 16, 64, 64])
    lwi = din("lwi", [NL, 16, 64, 64])
    c_tri = din("c_tri", [128, 128])
    c_triu = din("c_triu", [128, 128])
    c_ident = din("c_ident", [128, 128])
    c_mask = din("c_mask", [128, 896])
    pflag = din("pflag", [128, 2])
    NSB = 4 * SEQ + NB * 8 * 65
    NSF = NB * 8 + 8 + 256 + 8 + 32
    NSV = NB * 8 * 65
    snd_b = nc.dram_tensor("snd_b", [128, 4 * SEQ], BF16)
    rcv_b = nc.dram_tensor("rcv_b", [256, 4 * SEQ], BF16)
    snd_v = nc.dram_tensor("snd_v", [128, NSV], BF16)
    rcv_v = nc.dram_tensor("rcv_v", [256, NSV], BF16)
    snd_f = nc.dram_tensor("snd_f", [128, NSF], F32)
    rcv_f = nc.dram_tensor("rcv_f", [256, NSF], F32)
    outT = nc.dram_tensor("outT", [D, SEQ], F32, kind="ExternalOutput").ap()
    xs = nc.dram_tensor("xs", [D, SEQ], F32, kind="Internal").ap()
    dbg_out = {}
    if dbg:
        for nm, shp in [("d_hT", [128, 16 * 512]), ("d_fox", [128, 8 * 512]), ("d_gla", [128, 4 * 512]),
                        ("d_lru", [128, 8 * 512]), ("d_x1", [128, 16 * 512])]:
            dbg_out[nm] = nc.dram_tensor(nm, shp, F32, kind="ExternalOutput").ap()

    with contextlib.ExitStack() as es:
        S = Sched(nc, es)
        off = [SB_BASE]

        def sb(name, shape, dt):
            nb = int(np.prod(shape[1:])) * (4 if dt == F32 else 2)
            nb = (nb + 63) // 64 * 64
            assert off[0] + nb <= SB_END, ("SBUF overflow", name, off[0], nb)
            t = nc.alloc_sbuf_tensor_at(name, list(shape), dt, offset=off[0])
            off[0] += nb
            return t

        xT = sb("xTt", [128, 16, T], F32)
        hT = sb("hT", [128, 16, T], BF16)
        mix_fox = sb("mix_fox", [128, 8, T], BF16)
        mix_gla = sb("mix_gla", [128, 4, T], BF16)
        mix_lru = sb("mix_lru", [128, 8, T], BF16)
        kT = sb("kT", [128, 4, 2 * SEQ], BF16)
        vc = sb("vc", [128, NBLK, 8, 65], BF16)
        gk_tm = sb("gk_tm", [128, 4, 256], F32)
        gv_tm = sb("gv_tm", [128, 4, 512], BF16)
        call = sb("call", [128, NBLK, 8], F32)
        carry = sb("carry", [128, 8], F32)
        cref = sb("cref", [128, 8], F32)
        fb = sb("fb", [128, 8], F32)
        pf = sb("pf", [128, 2], F32)
        ones_bf = sb("ones_bf", [128, 128], BF16)
        ones_f = sb("ones_f", [128, 128], F32)
        tri_f = sb("tri_f", [128, 128], F32)
        triu_f = sb("triu_f", [128, 128], F32)
        ident_bf = sb("ident_bf", [128, 128], BF16)
        fmask = sb("fmask", [128, 896], BF16)
        P = sb("P", [128, NP], F32)
        GF = sb("GF", [128, 16], F32)
        w2 = sb("w2", [16, 256], BF16)
        gb = sb("gb", [1, 256], F32)
        Wa = sb("Wa", [128, 8, 128], BF16)
        Wi = sb("Wi", [128, 8, 128], BF16)
        cl = sb("cl", [128, 8], F32)
        hb = sb("hb", [128, 16], F32)
        S_f = sb("S_f", [128, 2, 128], F32)
        S_bf = sb("S_bf", [128, 2, 128], BF16)
        hst = sb("hst", [128, 8], F32)
        ctail = sb("ctail", [128, 8, 4], F32)
        rstd = sb("rstd", [128, T], F32)
        sq = [sb("sq%d" % i, [128, T], BF16) for i in range(2)]
        NA = 4
        WA = [sb("WA%d" % i, [128, 20, 128], BF16) for i in range(NA)]
        WB = [sb("WB%d" % i, [128, 16, 264], BF16) for i in range(2)]
        WD = [sb("WD%d" % i, [128, 22, 128], BF16) for i in range(2)]
        shared_base = off[0]
        off[0] = shared_base
        qT = [sb("qT%d" % i, [128, T], BF16) for i in range(2)]
        PT = [sb("PT%d" % i, [128, T], BF16) for i in range(4)]
        fbiasT = sb("fbiasT", [128, NBLK, 8], F32)
        osb = sb("osb", [128, T], F32)
        rrow = osb
        ffraw = sb("ffraw", [128, 4, 8], F32)
        spf = sb("spf", [128, 4, 8], F32)
        end_fox = off[0]
        lxbuf = [sb("lxbuf%d" % i, [128, 516], F32) for i in range(2)]
        xg = sb("xg", [128, T], F32)
        tA = sb("tA", [128, T], F32)
        tB = sb("tB", [128, T], F32)
        tC = sb("tC", [128, T], F32)
        tD = sb("tD", [128, T], F32)
        tE = sb("tE", [128, T], F32)
        glu = sb("glu", [128, T], F32)
        lin_bf = sb("lin_bf", [128, T], BF16)
        lssq = sb("lssq", [128, T], F32)
        lsq = sb("lsq", [128, T], BF16)
        end_lru = off[0]
        off[0] = shared_base
        grT = sb("grT", [128, T], BF16)
        spg = sb("spg", [128, 4, 256], F32)
        tmpE = sb("tmpE", [128, 1024], F32)
        Qt = sb("Qt", [128, 2, T], BF16)
        Kt = sb("Kt", [128, 2, T], BF16)
        Khat = sb("Khat", [128, 4, 256], BF16)
        eblast = sb("eblast", [128, 2, 4], F32)
        gsilu = sb("gsilu", [128, 4, T], BF16)
        gla_o = sb("gla_o", [128, 4, T], F32)
        ATb = [sb("AT%d" % i, [128, 128], BF16) for i in range(2)]
        end_gla = off[0]
        off[0] = shared_base
        act = sb("act", [128, 22, T], BF16)
        sg = [sb("sg%d" % i, [128, T], F32) for i in range(2)]
        ost = [sb("ost%d" % i, [128, T], F32) for i in range(2)]
        end_ffn = off[0]

        ps = [es.enter_context(nc.psum_tensor("ps%d" % i, [128, 512], F32)) for i in range(8)]
        wq = WQ(S, {"A": WA, "B": WB, "D": WD})

        def mm(out, lhsT, rhs, start, stop, reads, writes):
            S.op("pe", lambda e: e.matmul(out, lhsT, rhs, start=start, stop=stop), reads=reads, writes=writes)

        def actf(out, in_, func, reads, writes, **kw):
            S.op("act", lambda e: e.activation(out=out, in_=in_, func=func, **kw), reads=reads, writes=writes)

        xs_v = xs.rearrange("(kc p) s -> p kc s", p=128)
        xin_v = xT_in.rearrange("(kc p) s -> p kc s", p=128)
        out_v = outT.rearrange("(kc p) s -> p kc s", p=128)
        XK = [("xT", k) for k in range(16)]
        HK = [("hT", k) for k in range(16)]

        cc_sem = es.enter_context(nc.semaphore("cc_sem"))
        cc_cnt = [0]

        def emit():
            wq.reset()
            rr = [0]

            DB = [[0, 1, 2, 3]]

            def dbank():
                b = DB[0][rr[0] % len(DB[0])]
                rr[0] += 1
                return b

            S.dma("sp", [(tri_f[:], c_tri[:, :]), (triu_f[:], c_triu[:, :]), (GF[:], gfin[:, :])], "cst", writes=["tri_f", "triu_f", "GF"])
            S.dma("pool", [(ident_bf[:], c_ident[:, :]), (fmask[:], c_mask[:, :])], "cst2", writes=["ident_bf", "fmask"])
            S.op("dve", lambda e: e.memset(ones_f[:], 1.0), writes=["ones_f"])
            S.op("dve", lambda e: e.memset(ones_bf[:], 1.0), writes=["ones_bf"])
            S.dma("sp", [(pf[:], pflag[:, :])], "cst3", writes=["pf"])
            S.op("dve", lambda e: e.memset(vc[:], 0.0), writes=["vc1"] + [("vc", b_) for b_ in range(NBLK)])
            S.op("dve", lambda e: e.memset(vc[:, :, :, 64:65], 1.0), writes=["vc1"])
            S.op("dve", lambda e: e.memset(kT[:], 0.0), writes=[("kT", h_) for h_ in range(4)])
            S.op("dve", lambda e: e.memset(call[:], 0.0), writes=["call"])
            for tns, key in ((S_f, "S_f"), (S_bf, "S_bf"), (hst, "hst"), (ctail, "ctail"), (carry, "carry")):
                S.op("dve", lambda e, tns=tns: e.memset(tns[:], 0.0), writes=[key])
            cc_cnt[0] = 0
            S.op("dve", lambda e: e.memset(Wa[:], 0.0), writes=["Wa"])
            S.op("dve", lambda e: e.memset(Wi[:], 0.0), writes=["Wi"])

            def norm_to_hT(gt, gcol, Dn=D):
                for kc in range(16):
                    s_ = sq[kc % 2]
                    actf(s_[:], xT[:, kc, :], AF.Square, [("xT", kc)], [("sq", kc % 2)])
                    mm(ps[7][:, :], ones_bf[:], s_[:], kc == 0, kc == 15, [("sq", kc % 2), "ones_bf"], ["ps7"])
                actf(rstd[:], ps[7][:, :], AF.Ln, ["ps7"], ["rstd"], scale=1.0 / Dn, bias=EPS)
                actf(rstd[:], rstd[:], AF.Exp, ["rstd"], ["rstd"], scale=-0.5)

            def dense_fm(cols0, ncols, wsrc_v, evac):
                def fill(slot, c0=cols0, src=wsrc_v):
                    return [(slot[:, 0:16, :], src[FM_IDX[c0]])]
                w, wk = wq.req("A", fill)
                b = dbank()
                bk = "ps%d" % b
                for kc in range(16):
                    mm(ps[b][0:ncols, :], w[:, kc, 0:ncols], hT[:, kc, :], kc == 0, kc == 15, [wk, ("hT", kc)], [bk])
                evac(ps[b], bk)

            for l in range(NL):
                S.dma("sp", [(P[:], p128[l]), (fb[:], fbias[l].partition_broadcast(128)), (gb[0:1, :], gbias[l:l + 1, :])],
                      "par", writes=["P", "fb", "gb"])
                lwa_v = lwa[l].rearrange("(c e) d f -> e d c f", e=2)
                lwi_v = lwi[l].rearrange("(c e) d f -> e d c f", e=2)
                S.dma("pool", [(w2[:], gw2[l]), (Wa[0:64, :, 0:64], lwa_v[0]), (Wa[64:128, :, 64:128], lwa_v[1]),
                               (Wi[0:64, :, 0:64], lwi_v[0]), (Wi[64:128, :, 64:128], lwi_v[1])], "par2", writes=["w2", "Wa", "Wi"])
                actf(cl[:], P[:, C_LAM:C_LAM + 8], AF.Exp, ["P"], ["cl"], scale=-1.0)
                actf(cl[:], cl[:], AF.Ln, ["cl"], ["cl"], bias=1.0)
                S.op("dve", lambda e: e.tensor_scalar(out=cl[:], in0=cl[:], scalar1=-4.0, scalar2=None, op0=ALU.mult), reads=["cl"], writes=["cl"])
                S.op("dve", lambda e: e.tensor_scalar(out=hb[:], in0=P[:, C_BA:C_BA + 16], scalar1=0.5, scalar2=None, op0=ALU.mult), reads=["P"], writes=["hb"])
                for tns, key in ((S_f[:].rearrange("p a b -> p (a b)"), "S_f"), (hst[:], "hst"), (ctail[:].rearrange("p a b -> p (a b)"), "ctail"), (carry[:], "carry")):
                    S.op("dve", lambda e, tns=tns: e.tensor_scalar(out=tns, in0=tns, scalar1=pf[:, 0:1], scalar2=None, op0=ALU.mult), reads=[key, "pf"], writes=[key])
                actf(S_bf[:].rearrange("p a b -> p (a b)"), S_f[:].rearrange("p a b -> p (a b)"), AF.Copy, ["S_f"], ["S_bf"])
                w_in_v = w_in_fm[l]
                w_tm_v = w_in_tm[l]
                w_gate_v = w_gate_r[l]
                w_up_v = w_up_r[l]

                for t in range(NT):
                    t0 = t * T
                    nkb = NB + 4 * (t + 1)
                    kb0 = NB + 4 * t
                    src_v = xin_v if l == 0 else xs_v
                    for kc in range(16):
                        S.dma("sp", [(xT[:, kc, :], src_v[:, kc, t0:t0 + T])], ("xld", kc), reads=([("xs", t, kc)] if l > 0 else []), writes=[("xT", kc)])
                    norm_to_hT(P, C_GMIX)
                    for kc in range(16):
                        S.op("dve", lambda e: e.scalar_tensor_tensor(out=hT[:, kc, :], in0=xT[:, kc, :], scalar=P[:, C_GMIX + kc:C_GMIX + kc + 1],
                                                                      in1=rstd[:], op0=ALU.mult, op1=ALU.mult),
                             reads=[("xT", kc), "P", "rstd"], writes=[("hT", kc)])
                    S.barrier()
                    slabs = [("fvA", 256, [(0, O_FV, 256)]), ("fvB", 256, [(0, O_FV + 256, 256)]),
                             ("ffgk", 264, [(0, O_FF, 8), (8, O_GK, 256)]),
                             ("gvA", 256, [(0, O_GV, 256)]), ("gvB", 256, [(0, O_GV + 256, 256)])]
                    for si_, (nm, ncols, parts) in enumerate(slabs):
                        def fill(slot, si_=si_, wv=w_tm_v):
                            return [(slot[:, :, :], wv[si_])]
                        w, wk = wq.req("B", fill)
                        for j in range(4):
                            b = dbank()
                            bk = "ps%d" % b
                            blk = kb0 + j
                            for kc in range(16):
                                mm(ps[b][:, 0:ncols], hT[:, kc, j * 128:(j + 1) * 128], w[:, kc, 0:ncols], kc == 0, kc == 15, [wk, ("hT", kc)], [bk])
                            if nm in ("fvA", "fvB"):
                                h0 = 0 if nm == "fvA" else 4
                                actf(vc[:, blk, h0:h0 + 4, 0:64], ps[b][:, 0:256].rearrange("p (h d) -> p h d", d=64), AF.Copy, [bk], [("vc", blk)])
                            elif nm == "ffgk":
                                S.op("dve", lambda e: e.tensor_copy(out=ffraw[:, j, :], in_=ps[b][:, 0:8]), reads=[bk], writes=[("ffraw", j)])
                                actf(gk_tm[:, j, :], ps[b][:, 8:264], AF.Copy, [bk], [("gk_tm", j)])
                            else:
                                c0 = 0 if nm == "gvA" else 256
                                actf(gv_tm[:, j, c0:c0 + 256], ps[b][:, 0:256], AF.Copy, [bk], [("gv_tm", j)])
                    FR = [("ffraw", j) for j in range(4)]
                    for j in range(4):
                        S.op("dve", lambda e: e.tensor_tensor(out=spf[:, j, :], in0=ffraw[:, j, :], in1=fb[:], op=ALU.add), reads=[("ffraw", j), "fb"], writes=["spf"])
                    actf(spf[:], spf[:], AF.Exp, ["spf"], ["spf"], scale=-1.0)
                    actf(spf[:], spf[:], AF.Ln, ["spf"], ["spf"], bias=1.0)
                    for j in range(4):
                        for j2 in range(j):
                            mm(ps[7][:, j * 8:(j + 1) * 8], ones_f[:], spf[:, j2, :], j2 == 0, False, ["spf", "ones_f"], ["ps7"])
                        mm(ps[7][:, j * 8:(j + 1) * 8], tri_f[:], spf[:, j, :], j == 0, True, ["spf", "tri_f"], ["ps7"])
                    for j in range(4):
                        mm(ps[7][:, 32:40], ones_f[:], spf[:, j, :], j == 0, j == 3, ["spf", "ones_f"], ["ps7"])
                    for j in range(2):
                        mm(ps[7][:, 40:48], ones_f[:], spf[:, j, :], j == 0, j == 1, ["spf", "ones_f"], ["ps7"])
                    for j in range(4):
                        S.op("dve", lambda e: e.tensor_tensor(out=call[:, kb0 + j, :], in0=carry[:], in1=ps[7][:, j * 8:(j + 1) * 8], op=ALU.subtract),
                             reads=["carry", "ps7"], writes=["call"])
                    S.op("dve", lambda e: e.tensor_tensor(out=cref[:], in0=carry[:], in1=ps[7][:, 40:48], op=ALU.subtract), reads=["carry", "ps7"], writes=["cref"])
                    S.op("dve", lambda e: e.tensor_tensor(out=carry[:], in0=carry[:], in1=ps[7][:, 32:40], op=ALU.subtract), reads=["carry", "ps7"], writes=["carry"])
                    for h in range(8):
                        S.op("dve", lambda e: e.tensor_scalar(out=fbiasT[:, 0:nkb, h], in0=call[:, 0:nkb, h], scalar1=cref[:, h:h + 1], scalar2=-1.0,
                                                               op0=ALU.subtract, op1=ALU.mult), reads=["call", "cref"], writes=["fbiasT"])
                    S.op("dve", lambda e: e.tensor_scalar(out=fbiasT[:, 0:NB, :], in0=fbiasT[:, 0:NB, :], scalar1=pf[:, 1:2], scalar2=None, op0=ALU.add),
                         reads=["fbiasT", "pf"], writes=["fbiasT"])
                    def fox_gen():
                        for hp in range(4):
                            q_ = qT[hp % 2]
                            qk_ = ("qT", hp % 2)
                            dense_fm(O_FQ + hp * 128, 128, w_in_v, lambda p_, bk: actf(q_[:], p_[:, :], AF.Copy, [bk], [qk_]))
                            yield
                            dense_fm(O_FK + hp * 128, 128, w_in_v, lambda p_, bk: actf(kT[:, hp, SEQ + t0:SEQ + t0 + T], p_[:, :], AF.Copy, [bk], [("kT", hp)]))
                            yield
                            for e_ in range(2):
                                h = 2 * hp + e_
                                lo = e_ * 64

                                def qk(kb):
                                    sbk = 3 + kb % 3
                                    key = "ps%d" % sbk
                                    diag = kb >= kb0
                                    mm(ps[sbk][:, :], kT[lo:lo + 64, hp, kb * 128:(kb + 1) * 128], q_[lo:lo + 64, :], True, not diag, [("kT", hp), qk_], [key])
                                    if diag:
                                        r = kb - kb0
                                        mm(ps[sbk][:, :], ident_bf[:], fmask[:, 384 - r * 128:384 - r * 128 + 512], False, True, ["ident_bf", "fmask"], [key])
                                    actf(PT[kb % 4][:], ps[sbk][:, :], AF.Exp, [key, "fbiasT"], [("PT", kb % 4)], scale=0.125, bias=fbiasT[:, kb, h:h + 1])

                                def pv(kb):
                                    mm(ps[6][0:65, :], vc[:, kb, h, :], PT[kb % 4][:], kb == 0, kb == nkb - 1, [("vc", kb), "vc1", ("PT", kb % 4)], ["ps6"])

                                qk(0)
                                qk(1)
                                for kb in range(nkb):
                                    if kb + 2 < nkb:
                                        qk(kb + 2)
                                    pv(kb)
                                    yield
                                S.op("dve", lambda e: e.reciprocal(out=rrow[64:65, :], in_=ps[6][64:65, :]), reads=["ps6"], writes=["rrow"])
                                mm(ps[7][0:64, :], ones_f[64:65, 0:64], rrow[64:65, :], True, True, ["ones_f", "rrow"], ["ps7"])
                                actf(osb[0:64, :], ps[6][0:64, :], AF.Copy, ["ps6"], ["osb"])
                                S.op("dve", lambda e: e.tensor_tensor(out=mix_fox[0:64, h, :], in0=osb[0:64, :], in1=ps[7][0:64, :], op=ALU.mult),
                                     reads=["osb", "ps7"], writes=[("mix_fox", h)])
                                yield
                        for h in range(8):
                            s_ = sq[h % 2]
                            actf(s_[0:64, :], mix_fox[0:64, h, :], AF.Square, [("mix_fox", h)], [("sq", h % 2)])
                            mm(ps[7][0:64, :], ones_bf[0:64, 0:64], s_[0:64, :], h == 0, h == 7, [("sq", h % 2), "ones_bf"], ["ps7"])
                        yield
                        actf(rstd[0:64, :], ps[7][0:64, :], AF.Ln, ["ps7"], ["rstd"], scale=1.0 / 512, bias=EPS)
                        actf(rstd[0:64, :], rstd[0:64, :], AF.Exp, ["rstd"], ["rstd"], scale=-0.5)
                        for h in range(8):
                            S.op("dve", lambda e: e.scalar_tensor_tensor(out=mix_fox[0:64, h, :], in0=mix_fox[0:64, h, :], scalar=P[0:64, C_GFOX + h:C_GFOX + h + 1],
                                                                          in1=rstd[0:64, :], op0=ALU.mult, op1=ALU.mult),
                                 reads=[("mix_fox", h), "P", "rstd"], writes=[("mix_fox", h)])
                            yield

                    def lru_gen():
                        for c in range(8):
                            lb = lxbuf[c % 2]
                            lk = ("lxbuf", c % 2)
                            S.op("dve", lambda e: e.tensor_copy(out=lb[:, 0:3], in_=ctail[:, c, 0:3]), reads=["ctail"], writes=[lk])
                            dense_fm(O_LX + c * 128, 128, w_in_v, lambda p_, bk: actf(lb[:, 3:515], p_[:, :], AF.Copy, [bk], [lk]))
                            S.op("dve", lambda e: e.tensor_copy(out=ctail[:, c, 0:3], in_=lb[:, 512:515]), reads=[lk], writes=["ctail"])
                            yield
                            dense_fm(O_LG + c * 128, 128, w_in_v, lambda p_, bk: actf(xg[:], p_[:, :], AF.Identity, [bk], ["xg"], scale=0.5))
                            yield
                            actf(tA[:], xg[:], AF.Square, ["xg"], ["tA"])
                            S.op("dve", lambda e: e.tensor_scalar(out=tA[:], in0=tA[:], scalar1=4 * 0.044715, scalar2=1.0, op0=ALU.mult, op1=ALU.add), reads=["tA"], writes=["tA"])
                            S.op("dve", lambda e: e.tensor_tensor(out=tA[:], in0=tA[:], in1=xg[:], op=ALU.mult), reads=["tA", "xg"], writes=["tA"])
                            yield
                            actf(tA[:], tA[:], AF.Tanh, ["tA"], ["tA"], scale=1.5957691216057308)
                            S.op("dve", lambda e: e.scalar_tensor_tensor(out=glu[:], in0=tA[:], scalar=1.0, in1=xg[:], op0=ALU.add, op1=ALU.mult), reads=["tA", "xg"], writes=["glu"])
                            cw = C_CW + c * 4
                            S.op("dve", lambda e: e.tensor_scalar(out=tB[:], in0=lb[:, 0:512], scalar1=P[:, cw:cw + 1], scalar2=P[:, C_CB + c:C_CB + c + 1],
                                                                   op0=ALU.mult, op1=ALU.add), reads=[lk, "P"], writes=["tB"])
                            yield
                            for jj in range(1, 4):
                                S.op("dve", lambda e: e.scalar_tensor_tensor(out=tB[:], in0=lb[:, jj:jj + 512], scalar=P[:, cw + jj:cw + jj + 1], in1=tB[:],
                                                                              op0=ALU.mult, op1=ALU.add), reads=[lk, "P", "tB"], writes=["tB"])
                            actf(lin_bf[:], tB[:], AF.Copy, ["tB"], ["lin_bf"])
                            yield
                            yield
                            yield
                            yield
                            mm(ps[2][:, :], Wa[:, c, :], lin_bf[:], True, True, ["Wa", "lin_bf"], ["ps2"])
                            actf(tC[:], ps[2][:, :], AF.Tanh, ["ps2", "hb"], ["tC"], scale=0.5, bias=hb[:, c:c + 1])
                            yield
                            mm(ps[2][:, :], Wi[:, c, :], lin_bf[:], True, True, ["Wi", "lin_bf"], ["ps2"])
                            actf(tD[:], ps[2][:, :], AF.Tanh, ["ps2", "hb"], ["tD"], scale=0.5, bias=hb[:, 8 + c:9 + c])
                            yield
                            actf(tC[:], tC[:], AF.Exp, ["tC", "cl"], ["tC"], scale=cl[:, c:c + 1], bias=cl[:, c:c + 1])
                            actf(tE[:], tC[:], AF.Square, ["tC"], ["tE"])
                            actf(tE[:], tE[:], AF.Sqrt, ["tE"], ["tE"], scale=-1.0, bias=1.0)
                            yield
                            S.op("dve", lambda e: e.scalar_tensor_tensor(out=tE[:], in0=tE[:], scalar=0.5, in1=tB[:], op0=ALU.mult, op1=ALU.mult), reads=["tE", "tB"], writes=["tE"])
                            S.op("dve", lambda e: e.scalar_tensor_tensor(out=tD[:], in0=tD[:], scalar=1.0, in1=tE[:], op0=ALU.add, op1=ALU.mult), reads=["tD", "tE"], writes=["tD"])
                            S.op("dve", lambda e: e.tensor_tensor_scan(out=tE[:], data0=tC[:], data1=tD[:], initial=hst[:, c:c + 1], op0=ALU.mult, op1=ALU.add),
                                 reads=["tC", "tD", "hst"], writes=["tE"])
                            yield
                            S.op("dve", lambda e: e.tensor_copy(out=hst[:, c:c + 1], in_=tE[:, 511:512]), reads=["tE"], writes=["hst"])
                            S.op("dve", lambda e: e.tensor_tensor(out=tE[:], in0=tE[:], in1=glu[:], op=ALU.mult), reads=["tE", "glu"], writes=["tE"])
                            actf(mix_lru[:, c, :], tE[:], AF.Copy, ["tE"], [("mix_lru", c)])
                            actf(lsq[:], tE[:], AF.Square, ["tE"], ["lsq"])
                            yield
                            yield
                            yield
                            mm(ps[2][:, :], ones_bf[:], lsq[:], True, True, ["lsq", "ones_bf"], ["ps2"])
                            if c == 0:
                                S.op("dve", lambda e: e.tensor_copy(out=lssq[:], in_=ps[2][:, :]), reads=["ps2"], writes=["lssq"])
                            else:
                                S.op("dve", lambda e: e.tensor_tensor(out=lssq[:], in0=ps[2][:, :], in1=lssq[:], op=ALU.add), reads=["ps2", "lssq"], writes=["lssq"])
                            yield
                        actf(lssq[:], lssq[:], AF.Ln, ["lssq"], ["lssq"], scale=1.0 / 1024, bias=EPS)
                        actf(lssq[:], lssq[:], AF.Exp, ["lssq"], ["lssq"], scale=-0.5)
                        for c in range(8):
                            S.op("dve", lambda e: e.scalar_tensor_tensor(out=mix_lru[:, c, :], in0=mix_lru[:, c, :], scalar=P[:, C_GLRU + c:C_GLRU + c + 1], in1=lssq[:],
                                                                          op0=ALU.mult, op1=ALU.mult), reads=[("mix_lru", c), "P", "lssq"], writes=[("mix_lru", c)])
                            yield

                    DB[0] = [0, 1]
                    gens = [fox_gen(), lru_gen()]
                    ratio = [1, 1]
                    alive = [True, True]
                    while any(alive):
                        for gi in range(2):
                            for _ in range(ratio[gi]):
                                if alive[gi]:
                                    try:
                                        next(gens[gi])
                                    except StopIteration:
                                        alive[gi] = False
                    S.barrier()
                    DB[0] = [0, 1, 2, 3]
                    S.barrier()
                    dense_fm(O_GR, 16, w_in_v, lambda p_, bk: actf(grT[0:16, :], p_[0:16, :], AF.Copy, [bk], ["grT"]))
                    for j in range(4):
                        pb = 4 + j // 2
                        c0 = (j % 2) * 256
                        mm(ps[pb][:, c0:c0 + 256], grT[0:16, j * 128:(j + 1) * 128], w2[0:16, :], True, False, ["grT", "w2"], ["ps%d" % pb])
                        mm(ps[pb][:, c0:c0 + 256], ones_f[0:1, 0:128], gb[0:1, :], False, True, ["ones_f", "gb"], ["ps%d" % pb])
                    spg2 = spg[:].rearrange("p j c -> p (j c)")
                    for hf in range(2):
                        actf(spg2[:, hf * 512:(hf + 1) * 512], ps[4 + hf][:, :], AF.Exp, ["ps%d" % (4 + hf)], ["spg"], scale=-1.0)
                    actf(spg2, spg2, AF.Ln, ["spg"], ["spg"], bias=1.0)
                    for pr in range(2):
                        for j in range(4):
                            mm(ps[4 + pr][:, j * 128:(j + 1) * 128], spg[:, j, pr * 128:(pr + 1) * 128], tri_f[:], True, True, ["spg", "tri_f"], ["ps%d" % (4 + pr)])
                    for j in range(4):
                        pb = 6 + j // 2
                        c0 = (j % 2) * 256
                        mm(ps[pb][:, c0:c0 + 256], triu_f[:], spg[:, j, :], True, True, ["spg", "triu_f"], ["ps%d" % pb])
                    for pr in range(2):
                        actf(tmpE[:, pr * 512:(pr + 1) * 512], ps[4 + pr][:, :], AF.Exp, ["ps%d" % (4 + pr)], ["tmpE"], scale=-1.0 / 16)
                    for pr in range(2):
                        for j in range(4):
                            c_ = pr * 512 + j * 128 + 127
                            S.op("dve", lambda e: e.tensor_copy(out=eblast[:, pr, j:j + 1], in_=tmpE[:, c_:c_ + 1]), reads=["tmpE"], writes=["eblast"])
                    for pr in range(2):
                        dense_fm(O_GQ + pr * 128, 128, w_in_v, lambda p_, bk: S.op("dve", lambda e: e.scalar_tensor_tensor(
                            out=Qt[:, pr, :], in0=p_[:, :], scalar=0.125, in1=tmpE[:, pr * 512:(pr + 1) * 512], op0=ALU.mult, op1=ALU.mult),
                            reads=[bk, "tmpE"], writes=["Qt"]))
                    for pr in range(2):
                        actf(tmpE[:, pr * 512:(pr + 1) * 512], ps[4 + pr][:, :], AF.Exp, ["ps%d" % (4 + pr)], ["tmpE"], scale=1.0 / 16)
                    for pr in range(2):
                        dense_fm(O_GK + pr * 128, 128, w_in_v, lambda p_, bk: S.op("dve", lambda e: e.tensor_tensor(
                            out=Kt[:, pr, :], in0=p_[:, :], in1=tmpE[:, pr * 512:(pr + 1) * 512], op=ALU.mult), reads=[bk, "tmpE"], writes=["Kt"]))
                    for hf in range(2):
                        actf(tmpE[:, hf * 512:(hf + 1) * 512], ps[6 + hf][:, :], AF.Exp, ["ps%d" % (6 + hf)], ["tmpE"], scale=-1.0 / 16)
                    S.op("dve", lambda e: e.tensor_tensor(out=Khat[:].rearrange("p j c -> p (j c)"), in0=gk_tm[:].rearrange("p j c -> p (j c)"), in1=tmpE[:, :], op=ALU.mult),
                         reads=["tmpE"] + [("gk_tm", j) for j in range(4)], writes=["Khat"])
                    for h in range(4):
                        dense_fm(O_GG + h * 128, 128, w_in_v, lambda p_, bk: actf(gsilu[:, h, :], p_[:, :], AF.Silu, [bk], ["gsilu"]))
                    GV = [("gv_tm", j) for j in range(4)]
                    k_ = 0
                    for j in range(4):
                        js = slice(j * 128, (j + 1) * 128)
                        for pr in range(2):
                            for e_ in range(2):
                                h = 2 * pr + e_
                                lo = e_ * 64
                                a_ = ATb[k_ % 2]
                                ak = ("AT", k_ % 2)
                                c4 = (k_ % 4) * 128
                                k_ += 1
                                mm(ps[4][:, c4:c4 + 128], Kt[lo:lo + 64, pr, js], Qt[lo:lo + 64, pr, js], True, True, ["Kt", "Qt"], [("ps4", c4)])
                                S.op("dve", lambda e: e.tensor_tensor(out=a_[:], in0=ps[4][:, c4:c4 + 128], in1=tri_f[:], op=ALU.mult), reads=[("ps4", c4), "tri_f"], writes=[ak])
                                mm(ps[5][:, c4:c4 + 128], gv_tm[:, j, h * 128:(h + 1) * 128], a_[:], True, False, [("gv_tm", j), ak], [("ps5", c4)])
                                mm(ps[5][:, c4:c4 + 128], S_bf[lo:lo + 64, pr, :], Qt[lo:lo + 64, pr, js], False, True, ["S_bf", "Qt"], [("ps5", c4)])
                                actf(gla_o[:, h, js], ps[5][:, c4:c4 + 128], AF.Copy, [("ps5", c4)], [("gla_o", h)])
                                mm(ps[6][lo:lo + 64, pr * 128:(pr + 1) * 128], Khat[:, j, h * 64:(h + 1) * 64], gv_tm[:, j, h * 128:(h + 1) * 128], True, True,
                                   ["Khat", ("gv_tm", j)], [("ps6", pr)])
                            S.op("dve", lambda e: e.scalar_tensor_tensor(out=S_f[:, pr, :], in0=S_f[:, pr, :], scalar=eblast[:, pr, j:j + 1],
                                                                          in1=ps[6][:, pr * 128:(pr + 1) * 128], op0=ALU.mult, op1=ALU.add),
                                 reads=["S_f", "eblast", ("ps6", pr)], writes=["S_f"])
                            actf(S_bf[:, pr, :], S_f[:, pr, :], AF.Copy, ["S_f"], ["S_bf"])
                    for h in range(4):
                        s_ = sq[h % 2]
                        actf(s_[:], gla_o[:, h, :], AF.Square, [("gla_o", h)], [("sq", h % 2)])
                        mm(ps[7][:, :], ones_bf[:], s_[:], True, True, [("sq", h % 2), "ones_bf"], ["ps7"])
                        actf(rstd[:], ps[7][:, :], AF.Ln, ["ps7"], ["rstd"], scale=1.0 / 128, bias=EPS)
                        actf(rstd[:], rstd[:], AF.Exp, ["rstd"], ["rstd"], scale=-0.5)
                        S.op("dve", lambda e: e.scalar_tensor_tensor(out=gla_o[:, h, :], in0=gla_o[:, h, :], scalar=P[:, C_GGLA:C_GGLA + 1], in1=rstd[:],
                                                                      op0=ALU.mult, op1=ALU.mult), reads=[("gla_o", h), "P", "rstd"], writes=[("gla_o", h)])
                        S.op("dve", lambda e: e.tensor_tensor(out=mix_gla[:, h, :], in0=gla_o[:, h, :], in1=gsilu[:, h, :], op=ALU.mult),
                             reads=[("gla_o", h), "gsilu"], writes=[("mix_gla", h)])
                    S.barrier()
                    S.barrier()
                    if t == NT - 1 and l < NL - 1:
                        sb_v = snd_b.ap()
                        sv_v = snd_v.ap()
                        sf_v = snd_f.ap()
                        o1 = 4 * SEQ
                        S.dma("sp", [(sb_v[:, 0:o1].rearrange("p (h s) -> p h s", h=4), kT[:, :, SEQ:2 * SEQ]),
                                     (sv_v[:, :].rearrange("p (b h d) -> p b h d", b=NB, h=8), vc[:, NB:NBLK, :, :]),
                                     (sf_v[:, 0:NB * 8].rearrange("p (b h) -> p b h", h=8), call[:, NB:NBLK, :]),
                                     (sf_v[:, NB * 8:NB * 8 + 8], carry[:]),
                                     (sf_v[:, NB * 8 + 8:NB * 8 + 264].rearrange("p (a b) -> p a b", a=2), S_f[:]),
                                     (sf_v[:, NB * 8 + 264:NB * 8 + 272], hst[:]),
                                     (sf_v[:, NB * 8 + 272:NB * 8 + 304].rearrange("p (a b) -> p a b", a=8), ctail[:])],
                              "snd", reads=[("kT", h_) for h_ in range(4)] + [("vc", b_) for b_ in range(NBLK)] + ["call", "carry", "S_f", "hst", "ctail"],
                              writes=["snd"])
                        if not S.dry:
                            rg = [[2 * i_, 2 * i_ + 1] for i_ in range(ncores // 2)]
                            S._deps("pool", ["snd"], ["rcv"])
                            nc.gpsimd.collective_compute("AllGather", ALU.bypass, replica_groups=rg, ins=[snd_b.ap().opt()], outs=[rcv_b.ap().opt()]).then_inc(cc_sem)
                            nc.gpsimd.collective_compute("AllGather", ALU.bypass, replica_groups=rg, ins=[snd_v.ap().opt()], outs=[rcv_v.ap().opt()]).then_inc(cc_sem)
                            nc.gpsimd.collective_compute("AllGather", ALU.bypass, replica_groups=rg, ins=[snd_f.ap().opt()], outs=[rcv_f.ap().opt()]).then_inc(cc_sem)
                            cc_cnt[0] += 3
                            S._record((cc_sem, cc_cnt[0], "dma"), ["snd"], ["rcv"])
                        rb_v = rcv_b.ap()
                        rv_v = rcv_v.ap()
                        rf_v = rcv_f.ap()
                        S.dma("sp", [(kT[:, :, 0:SEQ], rb_v[0:128, 0:o1].rearrange("p (h s) -> p h s", h=4)),
                                     (vc[:, 0:NB, :, :], rv_v[0:128, :].rearrange("p (b h d) -> p b h d", b=NB, h=8)),
                                     (call[:, 0:NB, :], rf_v[0:128, 0:NB * 8].rearrange("p (b h) -> p b h", h=8)),
                                     (carry[:], rf_v[0:128, NB * 8:NB * 8 + 8]),
                                     (S_f[:], rf_v[0:128, NB * 8 + 8:NB * 8 + 264].rearrange("p (a b) -> p a b", a=2)),
                                     (hst[:], rf_v[0:128, NB * 8 + 264:NB * 8 + 272]),
                                     (ctail[:], rf_v[0:128, NB * 8 + 272:NB * 8 + 304].rearrange("p (a b) -> p a b", a=8))],
                              "rcv", reads=["rcv"],
                              writes=[("kT", h_) for h_ in range(4)] + [("vc", b_) for b_ in range(NB)] + ["call", "carry", "S_f", "hst", "ctail"])
                    if dbg and l == 0 and t == NT - 1:
                        def dump(nm, src, n):
                            for i in range(n):
                                actf(tA[:], src(i), AF.Copy, ["dbgsrc"], ["tA"])
                                S.dma("sp", [(dbg_out[nm][:, i * 512:(i + 1) * 512], tA[:])], "dbg", reads=["tA"])
                        dump("d_fox", lambda i: mix_fox[:, i, :], 8)
                        dump("d_gla", lambda i: mix_gla[:, i, :], 4)
                        dump("d_lru", lambda i: mix_lru[:, i, :], 8)
                        S.barrier()
                    MK = [("mix_fox", h) for h in range(8)] + [("mix_gla", h) for h in range(4)] + [("mix_lru", c) for c in range(8)]
                    for oc in range(16):
                        cs = slice(oc * 128, (oc + 1) * 128)

                        def fill(slot, oc=oc, l=l):
                            return [(slot[:, :, :], w_out_r[l, oc])]
                        w, wk = wq.req("A", fill)
                        b = dbank()
                        bk = "ps%d" % b
                        for h in range(8):
                            mm(ps[b][:, :], w[0:64, h, :], mix_fox[0:64, h, :], h == 0, False, [wk, ("mix_fox", h)], [bk])
                        for h in range(4):
                            mm(ps[b][:, :], w[:, 8 + h, :], mix_gla[:, h, :], False, False, [wk, ("mix_gla", h)], [bk])
                        for c in range(8):
                            mm(ps[b][:, :], w[:, 12 + c, :], mix_lru[:, c, :], False, c == 7, [wk, ("mix_lru", c)], [bk])
                        S.op("dve", lambda e: e.tensor_tensor(out=xT[:, oc, :], in0=ps[b][:, :], in1=xT[:, oc, :], op=ALU.add), reads=[bk, ("xT", oc)], writes=[("xT", oc)])
                    if dbg and l == 0 and t == NT - 1:
                        S.dma("sp", [(dbg_out["d_x1"].rearrange("p (k s) -> p k s", k=16), xT[:])], "dbg2", reads=XK)
                    norm_to_hT(P, C_GFFN)
                    for kc in range(16):
                        S.op("dve", lambda e: e.scalar_tensor_tensor(out=hT[:, kc, :], in0=xT[:, kc, :], scalar=P[:, C_GFFN + kc:C_GFFN + kc + 1],
                                                                      in1=rstd[:], op0=ALU.mult, op1=ALU.mult),
                             reads=[("xT", kc), "P", "rstd"], writes=[("hT", kc)])
                    for hh in range(2):
                        for hl in range(22):
                            hc = hh * 22 + hl
                            res = {}
                            for nm, wv in (("g", w_gate_v), ("u", w_up_v)):
                                def fill(slot, wv=wv, hc=hc):
                                    return [(slot[:, 0:16, :], wv[hc])]
                                w, wk = wq.req("A", fill)
                                b = dbank()
                                for kc in range(16):
                                    mm(ps[b][:, :], w[:, kc, :], hT[:, kc, :], kc == 0, kc == 15, [wk, ("hT", kc)], ["ps%d" % b])
                                res[nm] = b
                            s_ = sg[hl % 2]
                            actf(s_[:], ps[res["g"]][:, :], AF.Silu, ["ps%d" % res["g"]], [("sg", hl % 2)])
                            S.op("dve", lambda e: e.tensor_tensor(out=act[:, hl, :], in0=s_[:], in1=ps[res["u"]][:, :], op=ALU.mult),
                                 reads=[("sg", hl % 2), "ps%d" % res["u"]], writes=[("act", hl)])
                        for oc in range(16):
                            def fill(slot, hh=hh, oc=oc, l=l):
                                return [(slot[:, 0:22, :], w_down_r[l, hh, oc])]
                            w, wk = wq.req("D", fill)
                            b = dbank()
                            bk = "ps%d" % b
                            for hl in range(22):
                                mm(ps[b][:, :], w[:, hl, :], act[:, hl, :], hl == 0, hl == 21, [wk, ("act", hl)], [bk])
                            S.op("dve", lambda e: e.tensor_tensor(out=xT[:, oc, :], in0=ps[b][:, :], in1=xT[:, oc, :], op=ALU.add), reads=[bk, ("xT", oc)], writes=[("xT", oc)])
                    if l < NL - 1:
                        for kc in range(16):
                            S.dma("sp", [(xs_v[:, kc, t0:t0 + T], xT[:, kc, :])], ("xst", kc), reads=[("xT", kc)], writes=[("xs", t, kc)])
                    else:
                        norm_to_hT(GF, 0)
                        for kc in range(16):
                            o_ = ost[kc % 2]
                            S.op("dve", lambda e: e.scalar_tensor_tensor(out=o_[:], in0=xT[:, kc, :], scalar=GF[:, kc:kc + 1], in1=rstd[:], op0=ALU.mult, op1=ALU.mult),
                                 reads=[("xT", kc), "GF", "rstd"], writes=[("ost", kc % 2)])
                            S.dma("sp", [(outT[kc * 128:(kc + 1) * 128, t0:t0 + T], o_[:])], ("ost", kc % 2), reads=[("ost", kc % 2)])
                    S.barrier(dma_keys=[("ost", 0), ("ost", 1), "dbg", "dbg2"])

        S.dry = True
        emit()
        S.dry = False
        emit()
        S.finish()
    return nc


def host_inputs(inputs, NL):
    f = lambda a: np.ascontiguousarray(np.asarray(a, dtype=np.float32))
    p = np.zeros((NL, 128, NP), np.float32)
    p[:, :, C_GMIX:C_GMIX + 16] = f(inputs["norm_mix"])[:NL].reshape(NL, 16, 128).transpose(0, 2, 1)
    p[:, :, C_GFFN:C_GFFN + 16] = f(inputs["norm_ffn"])[:NL].reshape(NL, 16, 128).transpose(0, 2, 1)
    p[:, 0:64, C_GFOX:C_GFOX + 8] = f(inputs["fox_out_norm"])[:NL].reshape(NL, 8, 64).transpose(0, 2, 1)
    p[:, :, C_GGLA] = f(inputs["gla_head_norm"])[:NL]
    p[:, :, C_CW:C_CW + 32] = f(inputs["conv_w"])[:NL].reshape(NL, 4, 8, 128).transpose(0, 3, 2, 1).reshape(NL, 128, 32)
    for nm, c in (("conv_b", C_CB), ("lru_b_a", C_BA), ("lru_b_i", C_BI), ("lru_lambda", C_LAM), ("lru_out_norm", C_GLRU)):
        p[:, :, c:c + 8] = f(inputs[nm])[:NL].reshape(NL, 8, 128).transpose(0, 2, 1)
    k = np.arange(128)
    tri = (k[:, None] <= k[None, :]).astype(np.float32)
    triu = (k[:, None] > k[None, :]).astype(np.float32)
    jj = np.arange(896)
    mask = np.where((jj[None, :] - 384) >= k[:, None], 0.0, NEG).astype(np.float32)
    f32 = lambda a: np.asarray(a, dtype=np.float32)
    w_in = f32(inputs["w_in"])[:NL].reshape(NL, 16, 128, INW)
    fm = np.zeros((NL, NFM, 128, 16, 128), np.float32)
    for i, c0 in enumerate(FM_COLS):
        n = 16 if c0 == O_GR else 128
        fm[:, i, :, :, :n] = w_in[:, :, :, c0:c0 + n].transpose(0, 2, 1, 3)
    tm = np.zeros((NL, 5, 128, 16, 264), np.float32)
    for i, parts in enumerate(TM_SLABS):
        for (d0, s0, n) in parts:
            tm[:, i, :, :, d0:d0 + n] = w_in[:, :, :, s0:s0 + n].transpose(0, 2, 1, 3)
    w_out = f32(inputs["w_out"])[:NL]
    wo = np.zeros((NL, 16, 128, 20, 128), np.float32)
    wo[:, :, 0:64, 0:8, :] = w_out[:, 0:512].reshape(NL, 8, 64, 16, 128).transpose(0, 3, 2, 1, 4)
    wo[:, :, :, 8:12, :] = w_out[:, 512:1024].reshape(NL, 4, 128, 16, 128).transpose(0, 3, 2, 1, 4)
    wo[:, :, :, 12:20, :] = w_out[:, 1024:2048].reshape(NL, 8, 128, 16, 128).transpose(0, 3, 2, 1, 4)
    wg = f32(inputs["w_gate"])[:NL].reshape(NL, 16, 128, 44, 128).transpose(0, 3, 2, 1, 4)
    wu = f32(inputs["w_up"])[:NL].reshape(NL, 16, 128, 44, 128).transpose(0, 3, 2, 1, 4)
    wd = f32(inputs["w_down"])[:NL].reshape(NL, 2, 22, 128, 16, 128).transpose(0, 1, 4, 3, 2, 5)
    layered = {
        "w_in_fm": fm, "w_in_tm": tm, "w_out_r": wo, "w_gate_r": wg, "w_up_r": wu, "w_down_r": wd, "p128": p,
        "fbias": f(inputs["fox_f_bias"])[:NL], "gw2": f(inputs["gla_gate_w2"])[:NL], "gbias": f(inputs["gla_gate_bias"])[:NL],
        "lwa": f(inputs["lru_w_a"])[:NL], "lwi": f(inputs["lru_w_i"])[:NL],
    }
    consts = {
        "gfin": np.ascontiguousarray(f(inputs["final_norm"]).reshape(16, 128).T),
        "c_tri": tri, "c_triu": triu, "c_ident": np.eye(128, dtype=np.float32), "c_mask": mask,
    }
    return layered, consts


def shifted(layered, par):
    out = {}
    for k_, a in layered.items():
        z = np.zeros((1,) + a.shape[1:], np.float32)
        if k_ == "fbias":
            z[:] = 30.0
        out[k_] = np.concatenate([a, z], 0) if par == 0 else np.concatenate([z, a], 0)
    return out


_NC_CACHE = {}


def kernel(**inputs):
    x = np.asarray(inputs["x"], dtype=np.float32)
    B, SEQ, _ = x.shape
    NL = int(np.asarray(inputs["w_in"]).shape[0])
    SEQC = SEQ // 2
    layered, consts = host_inputs(inputs, NL)
    key = (NL + 1, SEQC, 2 * B)
    if key not in _NC_CACHE:
        _NC_CACHE[key] = build_nc(NL + 1, SEQC, ncores=2 * B)
    nc = _NC_CACHE[key]
    per_par = [shifted(layered, 0), shifted(layered, 1)]
    pflags = [np.tile(np.array([[0.0, NEG]], np.float32), (128, 1)), np.tile(np.array([[1.0, 0.0]], np.float32), (128, 1))]
    in_maps = []
    for b in range(B):
        for par in range(2):
            m = dict(consts)
            m.update(per_par[par])
            m["pflag"] = pflags[par]
            m["xT"] = np.ascontiguousarray(x[b, par * SEQC:(par + 1) * SEQC].T)
            in_maps.append(m)
    res = run_bass_kernel_spmd(nc, in_maps, core_ids=list(range(2 * B)))
    out = np.empty((B, SEQ, D), np.float32)
    for b in range(B):
        for par in range(2):
            out[b, par * SEQC:(par + 1) * SEQC] = np.asarray(res.results[2 * b + par]["outT"], dtype=np.float32).T
    return out
```
